# Optimizing a Trainium2 kernel written in Bass

```python
import jax, jax.numpy as jnp
from jax import lax
import numpy as np

D_MODEL = 1024
BATCH = 8
SEQ = 8192
DEPTH = 1
DEC_BATCH = 32
DEC_SEQ = 16
PAST_LEN = 1024

CHUNK = 64
D_A = 1024
D_B = 512
D_MIX = D_A + D_B
N_HEADS_A = 16
HEAD_DIM_A = D_A // N_HEADS_A
CONV_A = 4
CONV_B = 3
LRU_C = 8.0
EPS = 1e-6
D_PROJ = 2 * D_A + 4 * D_B
SPLITS = (D_A, 2 * D_A, 2 * D_A + D_B, 2 * D_A + 2 * D_B, 2 * D_A + 3 * D_B)

kernel_name = "hymba_rglru_shortconv_stream_step"


def _rmsnorm_f32(x, g):
    xf = x.astype(jnp.float32)
    r = lax.rsqrt(jnp.mean(xf * xf, axis=-1, keepdims=True) + EPS)
    return xf * r * g.astype(jnp.float32)


def _causal_dwconv(x, hist, w):
    W = w.shape[0]
    T = x.shape[1]
    xe = jnp.concatenate([hist.astype(x.dtype), x], axis=1)
    y = w[0] * xe[:, 0:T]
    for k in range(1, W):
        y = y + w[k] * xe[:, k:k + T]
    return y, xe[:, xe.shape[1] - (W - 1):]


def _lru_combine(left, right):
    a1, b1 = left
    a2, b2 = right
    return a1 * a2, a2 * b1 + b2


def _rg_lru(x, h0, w_rg, b_rg, w_ig, b_ig, lam):
    Bn, T, _ = x.shape
    xf = x.astype(jnp.float32)
    xh = xf.reshape(Bn, T, N_HEADS_A, HEAD_DIM_A)
    r = jax.nn.sigmoid(jnp.einsum('bthi,hij->bthj', xh, w_rg.astype(jnp.float32)).reshape(Bn, T, D_A) + b_rg)
    i = jax.nn.sigmoid(jnp.einsum('bthi,hij->bthj', xh, w_ig.astype(jnp.float32)).reshape(Bn, T, D_A) + b_ig)
    log_a = -LRU_C * r * jax.nn.softplus(-lam.astype(jnp.float32))
    a = jnp.exp(log_a)
    b = jnp.sqrt(-jnp.expm1(2.0 * log_a)) * (i * xf)
    b = b.at[:, 0].add(a[:, 0] * h0.astype(jnp.float32))
    _, h = lax.associative_scan(_lru_combine, (a, b), axis=1)
    return h, h[:, -1]


def _layer(x, c, conv_a_hist, h0, conv_b_hist,
           g_norm, w_ada, b_ada, w_in, w_conv_a, b_conv_a,
           w_rg, b_rg, w_ig, b_ig, lam, w_conv_b, w_out):
    dt = x.dtype
    ada = jax.nn.silu(c.astype(jnp.float32)) @ w_ada.astype(jnp.float32) + b_ada
    shift, scale, gate = jnp.split(ada, 3, axis=-1)
    hn = _rmsnorm_f32(x, g_norm) * (1.0 + scale[:, None]) + shift[:, None]
    proj = jnp.einsum('btd,dp->btp', hn.astype(dt), w_in)
    x_a, g_a, x_b, bgate, cgate, g_b = jnp.split(proj, SPLITS, axis=-1)
    xa_c, new_conv_a = _causal_dwconv(x_a, conv_a_hist, w_conv_a)
    h, h_last = _rg_lru(xa_c + b_conv_a, h0, w_rg, b_rg, w_ig, b_ig, lam)
    y_a = h.astype(dt) * jax.nn.silu(g_a)
    u, new_conv_b = _causal_dwconv(cgate * x_b, conv_b_hist, w_conv_b)
    y_b = bgate * u * jax.nn.silu(g_b)
    out = jnp.einsum('btm,md->btd', jnp.concatenate([y_a, y_b], axis=-1), w_out)
    x = (x.astype(jnp.float32) + gate[:, None] * out.astype(jnp.float32)).astype(dt)
    return x, new_conv_a, h_last.astype(h0.dtype), new_conv_b


def setup_inputs(seed: int = 0) -> dict:
    key = jax.random.key(seed)
    ks = jax.random.split(key, 24)
    f32 = jnp.float32
    nrm = lambda k, s, sc: jax.random.normal(k, s, f32) * sc
    a0 = jax.random.uniform(ks[15], (DEPTH, D_A), f32, 0.9, 0.999)
    return {
        "x_prompt": nrm(ks[0], (BATCH, SEQ, D_MODEL), 1.0),
        "x_sample": nrm(ks[1], (DEC_BATCH, DEC_SEQ, D_MODEL), 1.0),
        "c_prompt": nrm(ks[2], (BATCH, D_MODEL), 1.0),
        "c_sample": nrm(ks[3], (DEC_BATCH, D_MODEL), 1.0),
        "cache_conv_a": nrm(ks[4], (DEPTH, DEC_BATCH, CONV_A - 1, D_A), 1.0),
        "state_lru": nrm(ks[5], (DEPTH, DEC_BATCH, D_A), 0.5),
        "cache_conv_b": nrm(ks[6], (DEPTH, DEC_BATCH, CONV_B - 1, D_B), 1.0),
        "g_norm": 1.0 + nrm(ks[7], (DEPTH, D_MODEL), 0.01),
        "w_ada": nrm(ks[8], (DEPTH, D_MODEL, 3 * D_MODEL), 0.5 * D_MODEL ** -0.5),
        "b_ada": nrm(ks[9], (DEPTH, 3 * D_MODEL), 0.01),
        "w_in": nrm(ks[10], (DEPTH, D_MODEL, D_PROJ), D_MODEL ** -0.5),
        "w_conv_a": nrm(ks[11], (DEPTH, CONV_A, D_A), CONV_A ** -0.5),
        "b_conv_a": nrm(ks[12], (DEPTH, D_A), 0.01),
        "w_rg": nrm(ks[13], (DEPTH, N_HEADS_A, HEAD_DIM_A, HEAD_DIM_A), HEAD_DIM_A ** -0.5),
        "b_rg": nrm(ks[14], (DEPTH, D_A), 0.01),
        "w_ig": nrm(ks[16], (DEPTH, N_HEADS_A, HEAD_DIM_A, HEAD_DIM_A), HEAD_DIM_A ** -0.5),
        "b_ig": nrm(ks[17], (DEPTH, D_A), 0.01),
        "lam": jnp.log(a0) - jnp.log1p(-a0),
        "w_conv_b": nrm(ks[18], (DEPTH, CONV_B, D_B), CONV_B ** -0.5),
        "w_out": nrm(ks[19], (DEPTH, D_MIX, D_MODEL), D_MIX ** -0.5),
        "g_final": 1.0 + nrm(ks[20], (D_MODEL,), 0.01),
    }


def reference(x_prompt, x_sample, c_prompt, c_sample, cache_conv_a, state_lru, cache_conv_b,
              g_norm, w_ada, b_ada, w_in, w_conv_a, b_conv_a, w_rg, b_rg, w_ig, b_ig, lam,
              w_conv_b, w_out, g_final):
    xp, xs = x_prompt, x_sample
    ca_p, h_p, cb_p, ca_s, h_s, cb_s = [], [], [], [], [], []
    Bp = x_prompt.shape[0]
    dt = x_prompt.dtype
    for l in range(DEPTH):
        params = (g_norm[l], w_ada[l], b_ada[l], w_in[l], w_conv_a[l], b_conv_a[l],
                  w_rg[l], b_rg[l], w_ig[l], b_ig[l], lam[l], w_conv_b[l], w_out[l])
        xp, a1, h1, b1 = _layer(xp, c_prompt,
                                jnp.zeros((Bp, CONV_A - 1, D_A), dt),
                                jnp.zeros((Bp, D_A), dt),
                                jnp.zeros((Bp, CONV_B - 1, D_B), dt), *params)
        xs, a2, h2, b2 = _layer(xs, c_sample, cache_conv_a[l], state_lru[l], cache_conv_b[l], *params)
        ca_p.append(a1); h_p.append(h1); cb_p.append(b1)
        ca_s.append(a2); h_s.append(h2); cb_s.append(b2)
    y_prompt = _rmsnorm_f32(xp, g_final).astype(dt)
    y_sample = _rmsnorm_f32(xs, g_final).astype(x_sample.dtype)
    conv_a_prompt = jnp.stack(ca_p)
    lru_prompt = jnp.stack(h_p)
    conv_b_prompt = jnp.stack(cb_p)
    conv_a_sample = jnp.stack(ca_s)
    lru_sample = jnp.stack(h_s)
    conv_b_sample = jnp.stack(cb_s)
    return (y_prompt, y_sample, conv_a_prompt, lru_prompt, conv_b_prompt, conv_a_sample, lru_sample, conv_b_sample)
```

```python
import contextlib
import numpy as np
import concourse.bass as bass
import concourse.mybir as mybir
from concourse.bass_utils import run_bass_kernel_spmd

F32 = mybir.dt.float32
BF16 = mybir.dt.bfloat16
AF = mybir.ActivationFunctionType
ALU = mybir.AluOpType

D = 1024
KC = 8
NT = 512
EPS = 1e-6
NS = 4
TS = 16
NCORES = 8


class _Op:
    __slots__ = ("eng", "idx", "fn", "waits", "kn", "has_dep", "stream", "semval")


class Prog:
    CENG = ("pe", "act", "dve", "pool", "sp")

    def __init__(self, nc):
        self.nc = nc
        self.ops = {e: [] for e in self.CENG}
        self.streams = {}
        self.ccount = {}
        self.last_w = {}
        self.readers = {}
        self.known = {e: {} for e in self.CENG}
        self.opmap = {}
        self.eng_obj = {"pe": nc.tensor, "act": nc.scalar, "dve": nc.vector, "pool": nc.gpsimd, "sp": nc.sync}

    def _add(self, eng, fn, reads, writes, stream=None):
        op = _Op()
        op.eng = eng
        op.fn = fn
        op.stream = stream
        op.has_dep = False
        op.semval = None
        if stream is None:
            counter = eng
            n = self.ccount.get(eng, 0)
            self.ccount[eng] = n + 1
        else:
            counter = "dma:" + stream
            n = self.streams.get(stream, 0)
            self.streams[stream] = n + 1
        key = (counter, n)
        op.idx = key
        deps = set()
        for r in reads:
            w = self.last_w.get(r)
            if w is not None:
                deps.add(w)
        for r in writes:
            w = self.last_w.get(r)
            if w is not None:
                deps.add(w)
            for rd in self.readers.get(r, ()):
                deps.add(rd)
        kn = self.known[eng]
        best = {}
        deps = {((c, self.streams[c[4:]] - 1) if c.startswith("dma:") else (c, m)) for c, m in deps}
        for c, m in deps:
            if (c, m) == key:
                continue
            if c == "pe" and eng == "pe" and stream is None:
                continue
            if kn.get(c, -1) >= m:
                continue
            if best.get(c, -1) < m:
                best[c] = m
        op.waits = sorted(best.items())
        for c, m in op.waits:
            src = self.opmap[(c, m)]
            src.has_dep = True
            for c2, m2 in src.kn.items():
                if kn.get(c2, -1) < m2:
                    kn[c2] = m2
            if kn.get(c, -1) < m:
                kn[c] = m
        op.kn = dict(kn)
        self.opmap[key] = op
        self.ops[eng].append(op)
        for r in reads:
            self.readers.setdefault(r, []).append(key)
        for r in writes:
            self.last_w[r] = key
            self.readers[r] = []
        return op

    def op(self, eng, fn, reads=(), writes=()):
        return self._add(eng, fn, tuple(reads), tuple(writes), None)

    def dma(self, q, stream, fn, reads=(), writes=()):
        return self._add(q, fn, tuple(reads), tuple(writes), stream)

    def emit(self, final_streams=()):
        nc = self.nc
        with contextlib.ExitStack() as es:
            sems = {}
            for e in self.CENG:
                sems[e] = es.enter_context(nc.semaphore("s_" + e))
            for s in self.streams:
                sems["dma:" + s] = es.enter_context(nc.semaphore("d_" + s))
            for e in self.CENG:
                cnt = 0
                for o in self.ops[e]:
                    if o.stream is None:
                        if o.has_dep:
                            cnt += 1
                            o.semval = cnt
                    else:
                        o.semval = 16 * (o.idx[1] + 1)
            block = es.enter_context(nc.Block())

            def body(e):
                eng = self.eng_obj[e]
                for o in self.ops[e]:
                    for c, m in o.waits:
                        eng.wait_ge(sems[c], self.opmap[(c, m)].semval)
                    inst = o.fn()
                    if o.stream is not None:
                        inst.then_inc(sems["dma:" + o.stream], 16)
                    elif o.has_dep:
                        inst.then_inc(sems[e], 1)
                if e == "sp":
                    for s in final_streams:
                        n = self.streams.get(s, 0)
                        if n:
                            eng.wait_ge(sems["dma:" + s], 16 * n)

            @block.tensor
            def _(t):
                body("pe")

            @block.scalar
            def _(t):
                body("act")

            @block.vector
            def _(t):
                body("dve")

            @block.gpsimd
            def _(t):
                body("pool")

            @block.sync
            def _(t):
                body("sp")


class _Blk:
    pass


C_GN, C_WCA, C_BCA, C_BRG, C_BIG, C_LAM, C_WCB, C_BADA = 0, 8, 40, 48, 56, 64, 72, 84
NPV = 108
W_ORDER = []
for _j in range(4):
    W_ORDER += [2 * _j, 8 + 2 * _j, 2 * _j + 1, 9 + 2 * _j, 24 + _j, 16 + _j, 20 + _j, 28 + _j]
W_POS = {pc: i for i, pc in enumerate(W_ORDER)}


def build_nc(TP):
    NBP = TP // NT
    nc = bass.Bass("TRN2", target_bir_lowering=False)

    def din(name, shape):
        return nc.dram_tensor(name, shape, F32, kind="ExternalInput").ap()

    def dout(name, shape):
        return nc.dram_tensor(name, shape, F32, kind="ExternalOutput").ap()

    xp = din("xp", [TP, D])
    xs = din("xs", [NS * TS, D])
    c5 = din("c5", [5, D])
    cca = din("cca", [NS * 3, D])
    slru = din("slru", [NS, D])
    ccb = din("ccb", [NS * 2, 512])
    pv = din("pv", [NPV, 128])
    b_ada = din("b_ada", [1, 3 * D])
    g_fin = din("g_fin", [1, D])
    w_ada = din("w_ada", [D, 3 * D])
    w_in = din("w_in", [D, 4096])
    w_out = din("w_out", [1536, D])
    wrg = din("wrg", [128, 8, 128])
    wig = din("wig", [128, 8, 128])

    yp = dout("yp", [TP, D])
    ys = dout("ys", [NS * TS, D])
    o_cap = dout("o_cap", [3, D])
    o_lrp = dout("o_lrp", [1, D])
    o_cbp = dout("o_cbp", [2, 512])
    o_cas = dout("o_cas", [NS * 3, D])
    o_lrs = dout("o_lrs", [NS, D])
    o_cbs = dout("o_cbs", [NS * 2, 512])

    with contextlib.ExitStack() as es:
        def sb(name, shape, dt):
            return es.enter_context(nc.sbuf_tensor(name, shape, dt))

        def ps(name):
            return es.enter_context(nc.psum_tensor(name, [128, 512], F32))

        w_in_sb = sb("w_in_sb", [128, KC, 4096], BF16)
        w_out_sb = sb("w_out_sb", [128, 12, D], BF16)
        wg_sb = sb("wg_sb", [128, 2, 8, 128], BF16)
        idf = sb("idf", [128, 128], F32)
        prm = sb("prm", [128, NPV], F32)
        cst = sb("cst", [128, 40], F32)
        scl = sb("scl", [128, KC, 5], F32)
        shf = sb("shf", [128, KC, 5], F32)
        scT = sb("scT", [128, KC * 5], F32)
        gfin_bc = sb("gfin_bc", [128, D], F32)
        hist_p = [sb("hist_p%d" % i, [128, 8, 1, 3], F32) for i in range(2)]
        hst_p = sb("hst_p", [128, 8, 1], F32)
        vhist_p = [sb("vhist_p%d" % i, [128, 4, 1, 2], F32) for i in range(2)]
        hist_s = sb("hist_s", [128, 8, NS, 3], F32)
        hist_s2 = sb("hist_s2", [128, 8, NS, 3], F32)
        hst_s = sb("hst_s", [128, 8, NS], F32)
        vhist_s = sb("vhist_s", [128, 4, NS, 2], F32)
        vhist_s2 = sb("vhist_s2", [128, 4, NS, 2], F32)
        stats = sb("stats", [128, 4, 16], F32)
        mhalf = sb("mhalf", [128, 16], F32)
        small = sb("small", [128, 256], F32)

        NXIN = 4
        x_in = [sb("x_in%d" % i, [128, D], F32) for i in range(NXIN)]
        hnT = [sb("hnT%d" % i, [128, KC * NT], BF16) for i in range(2)]
        yT = [sb("yT%d" % i, [128, 12 * NT], BF16) for i in range(2)]
        NXR = 4
        XR = [sb("XR%d" % i, [128, D], F32) for i in range(NXR)]
        EXTW = 3 + NT

        class Rot:
            def __init__(self, name, n, width, dt):
                self.tiles = [sb("%s%d" % (name, i), [128, width], dt) for i in range(n)]
                self.name, self.i = name, 0

            def next(self):
                k = self.i % len(self.tiles)
                self.i += 1
                return self.tiles[k], "%s%d" % (self.name, k)

        R_ext = Rot("ext", 2, EXTW, F32)
        R_XC = Rot("XC", 2, NT, F32)
        R_XB = Rot("XB", 2, NT, BF16)
        class RotPair(Rot):
            def __init__(self, name):
                self.base = sb(name + "p", [128, 2 * NT], F32)
                self.tiles = [self.base[:, 0:NT], self.base[:, NT:2 * NT]]
                self.name, self.i = name, 0

            def pair(self, n):
                return self.base[:].rearrange("p (k t) -> p k t", k=2)[:, :, 0:n]

        R_AA = RotPair("AA")
        R_CG = Rot("CG", 1, NT, F32)
        R_II = RotPair("II")
        R_SS = RotPair("SS")
        R_UB = Rot("UB", 1, NT, F32)
        R_SG = Rot("SG", 3, NT, F32)
        R_SGB = Rot("SGB", 1, NT, F32)
        junk = R_CG.tiles[0][:].bitcast(BF16)

        T = ps("T")
        O = ps("O")
        PB = [ps("P%d" % i) for i in range(4)]
        GB = [ps("G%d" % i) for i in range(2)]

        P = Prog(nc)
        V, ACT, PE, PO = nc.vector, nc.scalar, nc.tensor, nc.gpsimd

        stg_tiles = [(XR[3], "XR3"), (XR[0], "XR0"), (XR[1], "XR1"), (XR[2], "XR2")]
        c5_sb, c5_n = x_in[0], "x_in0"
        bada_sb, bada_n = [x_in[1], x_in[2], x_in[3]], ["x_in1", "x_in2", "x_in3"]
        gate_bcP, gate_bcP_n = XR[3], "XR3"
        yT1f = yT[1][:].bitcast(F32)
        gate_bcS, gate_bcS_n = yT1f[:, 0:D], "yT1"
        st2_sb = yT1f[:, 2048:2560]

        ones_row = small[0:1, 0:128]
        selP = small[0:5, 0:128]
        selS = small[0:5, 128:192]
        P.op("pool", lambda: PO.memset(idf[:], 0.0), writes=["idf"])
        P.op("pool", lambda: PO.affine_select(out=idf[:], in_=idf[:], pattern=[[-1, 128]], compare_op=ALU.not_equal,
                                              fill=1.0, base=0, channel_multiplier=1), reads=["idf"], writes=["idf"])
        P.op("pool", lambda: PO.memset(small[:], 1.0), writes=["small"])
        P.op("pool", lambda: PO.affine_select(out=small[0:5, 0:128], in_=small[0:5, 0:128], pattern=[[0, 128]],
                                              compare_op=ALU.is_ge, fill=0.0, base=0, channel_multiplier=-1),
             reads=["small"], writes=["small"])
        P.op("pool", lambda: PO.affine_select(out=small[0:5, 128:192], in_=small[0:5, 128:192], pattern=[[1, 64]],
                                              compare_op=ALU.is_ge, fill=0.0, base=16, channel_multiplier=-16),
             reads=["small"], writes=["small"])
        P.op("pool", lambda: PO.affine_select(out=small[0:5, 128:192], in_=small[0:5, 128:192], pattern=[[-1, 64]],
                                              compare_op=ALU.is_ge, fill=0.0, base=-1, channel_multiplier=16),
             reads=["small"], writes=["small"])
        P.op("pool", lambda: PO.memset(mhalf[:], -0.5), writes=["mhalf"])
        P.op("pool", lambda: PO.memset(hist_p[0][:], 0.0), writes=["hist_p0"])
        P.op("pool", lambda: PO.memset(hst_p[:], 0.0), writes=["hst_p"])
        P.op("pool", lambda: PO.memset(vhist_p[0][:], 0.0), writes=["vhist_p0"])

        pv_sb, pv_n = hnT[1][:].bitcast(F32), "hnT1"
        P.dma("sp", "ld_pv", lambda: nc.sync.dma_start(out=pv_sb[0:NPV, 0:128], in_=pv), writes=[pv_n])
        P.dma("sp", "ld_c5", lambda: nc.sync.dma_start(out=c5_sb[0:5, :], in_=c5), writes=[c5_n])
        for s in range(5):
            for t3 in range(3):
                P.dma("sp", "ld_bada", (lambda s=s, t3=t3: nc.sync.dma_start(
                    out=bada_sb[t3][s:s + 1, :], in_=b_ada[0:1, t3 * D:(t3 + 1) * D])), writes=[bada_n[t3]])
        gf_sb, gf_n = hnT[1][:].bitcast(F32), "hnT1"
        P.dma("sp", "ld_gf", lambda: nc.sync.dma_start(out=gf_sb[0:1, 1024:2048], in_=g_fin), writes=[gf_n])
        st_sb = hnT[0][:].bitcast(F32)
        P.dma("sp", "ld_st", lambda: nc.sync.dma_start(out=st_sb[0:NS * 3, 0:1024], in_=cca), writes=["hnT0"])
        P.dma("sp", "ld_st", lambda: nc.sync.dma_start(out=st_sb[0:NS, 1024:2048], in_=slru), writes=["hnT0"])
        P.dma("sp", "ld_st", lambda: nc.sync.dma_start(out=st2_sb[0:NS * 2, 0:512], in_=ccb), writes=["yT1"])

        P.dma("pool", "ld_wg", lambda: PO.dma_start(out=wg_sb[:, 0, :, :], in_=wrg), writes=["wg"])
        P.dma("pool", "ld_wg", lambda: PO.dma_start(out=wg_sb[:, 1, :, :], in_=wig), writes=["wg"])

        P.op("pe", lambda: PE.transpose(out=T[:, 0:NPV], in_=pv_sb[0:NPV, 0:128], identity=idf[0:NPV, 0:NPV]),
             reads=[pv_n, "idf"], writes=["T"])
        P.op("dve", lambda: V.tensor_copy(out=prm[:], in_=T[:, 0:NPV]), reads=["T"], writes=["prm"])
        P.op("dve", lambda: V.tensor_scalar(out=cst[:, 0:16], in0=prm[:, C_BRG:C_BRG + 16], scalar1=0.5, scalar2=None,
                                            op0=ALU.mult), reads=["prm"], writes=["cst_hb"])
        P.op("act", lambda: ACT.activation(out=cst[:, 24:32], in_=prm[:, C_LAM:C_LAM + 8], func=AF.Exp, scale=-1.0),
             reads=["prm"], writes=["cst_t"])
        P.op("act", lambda: ACT.activation(out=cst[:, 32:40], in_=cst[:, 24:32], func=AF.Ln, bias=1.0),
             reads=["cst_t"], writes=["cst_t2"])
        P.op("dve", lambda: V.tensor_scalar(out=cst[:, 16:24], in0=cst[:, 32:40], scalar1=-4.0, scalar2=None,
                                            op0=ALU.mult), reads=["cst_t2"], writes=["cst_hc"])

        for kc in range(KC):
            P.op("pe", (lambda kc=kc: PE.transpose(out=T[:, 128 + kc * 5:128 + (kc + 1) * 5],
                                                   in_=c5_sb[0:5, kc * 128:(kc + 1) * 128], identity=idf[0:5, 0:5])),
                 reads=[c5_n, "idf"], writes=["T"])
        P.op("act", lambda: ACT.activation(out=scT[:], in_=T[:, 128:168], func=AF.Silu), reads=["T"], writes=["scT"])

        abanks = [(PB[0], "P0"), (PB[1], "P1"), (PB[2], "P2"), (PB[3], "P3"), (GB[0], "G0"), (GB[1], "G1")]
        spool = [(t, n, D) for t, n in stg_tiles]
        for R_ in (R_XC, R_AA, R_II, R_SS, R_SG):
            spool += [(t, "%s%d" % (R_.name, i), NT) for i, t in enumerate(R_.tiles)]
        piece = 0
        for kc in range(KC):
            col = 0
            while col < 3 * D:
                stg, stg_n, cap = spool[piece % len(spool)]
                piece += 1
                wdt = min(cap, 3 * D - col)
                P.dma("sp", "ld_wada_" + stg_n, (lambda kc=kc, col=col, wdt=wdt, stg=stg: nc.sync.dma_start(
                    out=stg[:, 0:wdt], in_=w_ada[kc * 128:(kc + 1) * 128, col:col + wdt])), writes=[stg_n])
                for hh in range(wdt // 512):
                    n6 = (col + hh * 512) // 512
                    bk, bk_n = abanks[n6]
                    last = (col + (hh + 1) * 512 == 3 * D)
                    tok = ["tok_win"] if (kc == 7 and last) else []
                    P.op("pe", (lambda kc=kc, hh=hh, stg=stg, bk=bk: PE.matmul(
                        bk[0:5, :], lhsT=scT[:, kc * 5:(kc + 1) * 5], rhs=stg[:, hh * 512:(hh + 1) * 512],
                        start=(kc == 0), stop=(kc == KC - 1))), reads=["scT", stg_n], writes=[bk_n] + tok)
                col += wdt
            if kc == 7:
                w_in3 = w_in.rearrange("(kc p) n -> p kc n", p=128)
                for g in range(8):
                    P.dma("pool", "ld_win%d" % g, (lambda g=g: PO.dma_start(
                        out=w_in_sb[:, :, g * 512:(g + 1) * 512], in_=w_in3[:, :, g * 512:(g + 1) * 512])),
                        reads=["tok_win"], writes=["w_in_g%d" % g])
        for c in range(12):
            P.dma("pool", "ld_wout", (lambda c=c: PO.dma_start(out=w_out_sb[:, c, :], in_=w_out[c * 128:(c + 1) * 128, :])),
                  writes=["w_out"])

        ada_sb, ada_n = yT[0][:].bitcast(F32), "yT0"
        for n6 in range(6):
            bk, bk_n = abanks[n6]
            t3, hh = divmod(n6, 2)
            P.op("dve", (lambda n6=n6, bk=bk, t3=t3, hh=hh: V.tensor_tensor(
                out=ada_sb[0:5, n6 * 512:(n6 + 1) * 512], in0=bk[0:5, :],
                in1=bada_sb[t3][0:5, hh * 512:(hh + 1) * 512], op=ALU.add)),
                reads=[bk_n, bada_n[t3]], writes=[ada_n])
        for j in range(16):
            P.op("pe", (lambda j=j: PE.transpose(out=T[:, 256 + j * 5:256 + (j + 1) * 5],
                                                 in_=ada_sb[0:5, j * 128:(j + 1) * 128], identity=idf[0:5, 0:5])),
                 reads=[ada_n, "idf"], writes=["T"])
        P.op("dve", lambda: V.tensor_copy(out=shf[:].rearrange("p k s -> p (k s)"), in_=T[:, 256:296]),
             reads=["T"], writes=["shf"])
        for kc in range(KC):
            P.op("dve", (lambda kc=kc: V.tensor_scalar(out=scl[:, kc, :], in0=T[:, 296 + kc * 5:296 + (kc + 1) * 5],
                                                       scalar1=1.0, scalar2=prm[:, C_GN + kc:C_GN + kc + 1],
                                                       op0=ALU.add, op1=ALU.mult)),
                 reads=["T", "prm"], writes=["scl"])
        gbanks = [(O, "O"), (PB[0], "P0"), (PB[1], "P1"), (PB[2], "P2"), (PB[3], "P3"), (GB[0], "G0")]
        for hh in range(2):
            (b0, b0n), (b1, b1n), (b2, b2n) = gbanks[3 * hh:3 * hh + 3]
            P.op("pe", (lambda hh=hh, b0=b0: PE.matmul(b0[:, :], lhsT=selP, rhs=ada_sb[0:5, 2048 + hh * 512:2048 + (hh + 1) * 512],
                                                       start=True, stop=True)), reads=["small", ada_n], writes=[b0n])
            P.op("pe", (lambda hh=hh, b1=b1: PE.matmul(b1[0:64, :], lhsT=selS, rhs=ada_sb[0:5, 2048 + hh * 512:2048 + (hh + 1) * 512],
                                                       start=True, stop=True)), reads=["small", ada_n], writes=[b1n])
            P.op("pe", (lambda hh=hh, b2=b2: PE.matmul(b2[:, :], lhsT=ones_row, rhs=gf_sb[0:1, 1024 + hh * 512:1024 + (hh + 1) * 512],
                                                       start=True, stop=True)), reads=["small", gf_n], writes=[b2n])
            P.op("dve", (lambda hh=hh, b0=b0: V.tensor_copy(out=gate_bcP[:, hh * 512:(hh + 1) * 512], in_=b0[:, :])),
                 reads=[b0n], writes=[gate_bcP_n])
            P.op("dve", (lambda hh=hh, b1=b1: V.tensor_copy(out=gate_bcS[0:64, hh * 512:(hh + 1) * 512], in_=b1[0:64, :])),
                 reads=[b1n], writes=[gate_bcS_n])
            P.op("dve", (lambda hh=hh, b2=b2: V.tensor_copy(out=gfin_bc[:, hh * 512:(hh + 1) * 512], in_=b2[:, :])),
                 reads=[b2n], writes=["gfin_bc"])
        for c in range(8):
            P.op("pe", (lambda c=c: PE.transpose(out=T[:, c * 12:(c + 1) * 12], in_=st_sb[0:12, c * 128:(c + 1) * 128],
                                                 identity=idf[0:12, 0:12])), reads=["hnT0", "idf"], writes=["T"])
        P.op("dve", lambda: V.tensor_copy(out=hist_s[:].rearrange("p c s k -> p (c s k)"), in_=T[:, 0:96]),
             reads=["T"], writes=["hist_s"])
        for c in range(8):
            P.op("pe", (lambda c=c: PE.transpose(out=T[:, 96 + c * 4:96 + (c + 1) * 4],
                                                 in_=st_sb[0:4, 1024 + c * 128:1024 + (c + 1) * 128],
                                                 identity=idf[0:4, 0:4])), reads=["hnT0", "idf"], writes=["T"])
        P.op("dve", lambda: V.tensor_copy(out=hst_s[:].rearrange("p c s -> p (c s)"), in_=T[:, 96:128]),
             reads=["T"], writes=["hst_s"])
        for c in range(4):
            P.op("pe", (lambda c=c: PE.transpose(out=T[:, 128 + c * 8:128 + (c + 1) * 8],
                                                 in_=st2_sb[0:8, c * 128:(c + 1) * 128],
                                                 identity=idf[0:8, 0:8])), reads=["yT1", "idf"], writes=["T"])
        P.op("dve", lambda: V.tensor_copy(out=vhist_s[:].rearrange("p c s k -> p (c s k)"), in_=T[:, 128:160]),
             reads=["T"], writes=["vhist_s"])

        blocks = []
        B = _Blk()
        B.name, B.is_sample = "s", True
        B.ntok, B.nseq, B.ts, B.np_, B.ntt = NS * TS, NS, TS, NS * TS, 1
        B.xsrc, B.ydst, B.row0 = xs, ys, 0
        B.seq0 = 1
        B.hin, B.hin_n, B.hout, B.hout_n = hist_s, "hist_s", hist_s2, "hist_s2"
        B.vin, B.vin_n, B.vout, B.vout_n = vhist_s, "vhist_s", vhist_s2, "vhist_s2"
        B.hst, B.hst_n = hst_s, "hst_s"
        blocks.append(B)
        for b in range(NBP):
            B = _Blk()
            B.name, B.is_sample = "p%d" % b, False
            B.ntok, B.nseq, B.ts, B.np_, B.ntt = NT, 1, NT, 128, NT // 128
            B.xsrc, B.ydst, B.row0 = xp, yp, b * NT
            B.seq0 = 0
            B.hin, B.hin_n, B.hout, B.hout_n = hist_p[b % 2], "hist_p%d" % (b % 2), hist_p[(b + 1) % 2], "hist_p%d" % ((b + 1) % 2)
            B.vin, B.vin_n, B.vout, B.vout_n = vhist_p[b % 2], "vhist_p%d" % (b % 2), vhist_p[(b + 1) % 2], "vhist_p%d" % ((b + 1) % 2)
            B.hst, B.hst_n = hst_p, "hst_p"
            blocks.append(B)
        for i, B in enumerate(blocks):
            B.buf = i % 2

        cnt = {"xin": 0, "col": 0, "pb": 0, "xr": 0}
        ywritten = set()

        def v3(ap, nseq):
            return ap.rearrange("p (s t) -> p s t", s=nseq)

        def xr_next():
            sl = cnt["xr"] % NXR
            cnt["xr"] += 1
            return sl

        def in_load(B):
            B.xslots = []
            for tt in range(B.ntt):
                sl = cnt["xin"] % NXIN
                cnt["xin"] += 1
                B.xslots.append(sl)
                np_ = B.np_
                xn = "x_in%d" % sl
                r0 = B.row0 + tt * 128
                P.dma("sp", "ld_x%d" % sl, (lambda sl=sl, r0=r0, np_=np_, B=B: nc.sync.dma_start(
                    out=x_in[sl][0:np_, :], in_=B.xsrc[r0:r0 + np_, :])), writes=[xn])

        def in_norm(B):
            for tt in range(B.ntt):
                sl = B.xslots[tt]
                col = cnt["col"] % 16
                cnt["col"] += 1
                np_ = B.np_
                xn = "x_in%d" % sl
                P.op("act", (lambda sl=sl, np_=np_, col=col: ACT.activation(
                    out=junk[0:np_, :], in_=x_in[sl][0:np_, :], func=AF.Square, accum_out=stats[0:np_, 0, col:col + 1])),
                    reads=[xn], writes=["CG0", "ssq%d" % col])
                P.op("dve", (lambda np_=np_, col=col: V.tensor_scalar(
                    out=stats[0:np_, 1, col:col + 1], in0=stats[0:np_, 0, col:col + 1], scalar1=1.0 / D, scalar2=EPS,
                    op0=ALU.mult, op1=ALU.add)), reads=["ssq%d" % col], writes=["msq%d" % col])
                P.op("pool", (lambda np_=np_, col=col: PO.tensor_tensor(
                    out=stats[0:np_, 2, col:col + 1], in0=stats[0:np_, 1, col:col + 1], in1=mhalf[0:np_, col:col + 1],
                    op=ALU.pow)), reads=["msq%d" % col, "mhalf"], writes=["rstd%d" % col])
                P.op("pool", (lambda sl=sl, np_=np_, col=col: PO.tensor_scalar(
                    out=x_in[sl][0:np_, :], in0=x_in[sl][0:np_, :], scalar1=stats[0:np_, 2, col:col + 1], scalar2=0.0,
                    op0=ALU.mult, op1=ALU.add)), reads=[xn, "rstd%d" % col], writes=[xn])

        T0 = T

        def t_group(B, kc, bank=None):
            T, T_n = bank if bank is not None else (T0, "T")
            hn = "hnT%d_%d" % (B.buf, kc)
            for tt in range(B.ntt):
                sl = B.xslots[tt]
                np_ = B.np_
                P.op("pe", (lambda sl=sl, np_=np_, tt=tt, kc=kc: PE.transpose(
                    out=T[:, tt * 128:tt * 128 + np_], in_=x_in[sl][0:np_, kc * 128:(kc + 1) * 128],
                    identity=idf[0:np_, 0:np_])), reads=["x_in%d" % sl, "idf"], writes=[T_n])
            for s in range(B.nseq):
                q = B.seq0 + s
                P.op("act", (lambda kc=kc, s=s, q=q, B=B: ACT.activation(
                    out=hnT[B.buf][:, kc * NT + s * B.ts:kc * NT + (s + 1) * B.ts], in_=T[:, s * B.ts:(s + 1) * B.ts],
                    func=AF.Identity, scale=scl[:, kc, q:q + 1], bias=shf[:, kc, q:q + 1])),
                    reads=[T_n, "scl", "shf"], writes=[hn, "hnT%d" % B.buf])

        def ip(B, pchunk):
            k = cnt["pb"] % 4
            cnt["pb"] += 1
            bk, bk_n = PB[k], "P%d" % k
            pos = W_POS[pchunk]
            for kc in range(KC):
                P.op("pe", (lambda kc=kc, bk=bk, B=B, pos=pos: PE.matmul(
                    bk[:, 0:B.ntok], lhsT=w_in_sb[:, kc, pos * 128:(pos + 1) * 128],
                    rhs=hnT[B.buf][:, kc * NT:kc * NT + B.ntok], start=(kc == 0), stop=(kc == KC - 1))),
                    reads=["w_in_g%d" % (pos // 4), "hnT%d_%d" % (B.buf, kc)], writes=[bk_n])
            return bk, bk_n

        class Ctx:
            pass

        def a_s1(B, j):
            c = Ctx()
            c.B, c.j = B, j
            n, ns, ts = B.ntok, B.nseq, B.ts
            W = ns * (3 + ts)
            ext, en = R_ext.next()
            c.XC, c.xcn = R_XC.next()
            c.XB, c.xbn = R_XB.next()
            c.SG, c.sgn = R_SG.next()
            XC, XB, SG = c.XC, c.XB, c.SG
            e3 = v3(ext[:, 0:W], ns)
            enh, enb = en + "h", en + "b"
            bx, bxn = ip(B, j)
            bg, bgn = ip(B, 8 + j)
            bx3 = v3(bx[:, 0:n], ns)
            P.op("act", lambda: ACT.activation(out=e3[:, :, 3:3 + ts], in_=bx3, func=AF.Copy), reads=[bxn], writes=[enb])
            P.op("act", lambda: ACT.activation(out=XC[:, 0:n], in_=bx[:, 0:n], func=AF.Identity,
                                               scale=prm[:, C_WCA + 24 + j:C_WCA + 25 + j],
                                               bias=prm[:, C_BCA + j:C_BCA + j + 1]),
                 reads=[bxn, "prm"], writes=[c.xcn])
            P.op("act", lambda: ACT.activation(out=B.hout[:, j, :, :], in_=bx3[:, :, ts - 3:ts], func=AF.Copy),
                 reads=[bxn], writes=[B.hout_n])
            P.op("act", lambda: ACT.activation(out=SG[:, 0:n], in_=bg[:, 0:n], func=AF.Silu), reads=[bgn], writes=[c.sgn])
            P.op("dve", lambda: V.tensor_copy(out=e3[:, :, 0:3], in_=B.hin[:, j, :, :]), reads=[B.hin_n], writes=[enh])
            for k in (2, 1, 0):
                P.op("dve", (lambda k=k: V.scalar_tensor_tensor(
                    out=v3(XC[:, 0:n], ns), in0=e3[:, :, k:k + ts], scalar=prm[:, C_WCA + k * 8 + j:C_WCA + k * 8 + j + 1],
                    in1=v3(XC[:, 0:n], ns), op0=ALU.mult, op1=ALU.add)), reads=[enh, enb, c.xcn, "prm"], writes=[c.xcn])
            P.op("dve", lambda: V.tensor_copy(out=XB[:, 0:n], in_=XC[:, 0:n]), reads=[c.xcn], writes=[c.xbn])
            return c

        def a_gates(c):
            j, n = c.j, c.B.ntok
            XB = c.XB
            P.op("pe", lambda: PE.matmul(GB[0][:, 0:n], lhsT=wg_sb[:, 0, j, :], rhs=XB[:, 0:n], start=True, stop=True),
                 reads=["wg", c.xbn], writes=["G0"])
            P.op("pe", lambda: PE.matmul(GB[1][:, 0:n], lhsT=wg_sb[:, 1, j, :], rhs=XB[:, 0:n], start=True, stop=True),
                 reads=["wg", c.xbn], writes=["G1"])

        def a_s2a(c):
            j, n = c.j, c.B.ntok
            c.AA, c.aan = R_AA.next()
            c.II, c.iin = R_II.next()
            c.SS, c.ssn = R_SS.next()
            AA, II, SS, XC = c.AA, c.II, c.SS, c.XC
            P.op("act", lambda: ACT.activation(out=AA[:, 0:n], in_=GB[0][:, 0:n], func=AF.Tanh, scale=0.5,
                                               bias=cst[:, j:j + 1]), reads=["G0", "cst_hb"], writes=[c.aan])
            P.op("act", lambda: ACT.activation(out=II[:, 0:n], in_=GB[1][:, 0:n], func=AF.Tanh, scale=0.5,
                                               bias=cst[:, 8 + j:9 + j]), reads=["G1", "cst_hb"], writes=[c.iin])
            P.op("dve", lambda: V.scalar_tensor_tensor(out=II[:, 0:n], in0=II[:, 0:n], scalar=1.0, in1=XC[:, 0:n],
                                                       op0=ALU.add, op1=ALU.mult), reads=[c.iin, c.xcn], writes=[c.iin])

        def a_exp(c):
            j, n = c.j, c.B.ntok
            AA = c.AA
            P.op("act", lambda: ACT.activation(out=AA[:, 0:n], in_=AA[:, 0:n], func=AF.Exp,
                                               scale=cst[:, 16 + j:17 + j], bias=cst[:, 16 + j:17 + j]),
                 reads=[c.aan, "cst_hc"], writes=[c.aan])

        def a_sq_pair(cs):
            n = cs[0].B.ntok
            assert [c.aan for c in cs] == ["AA0", "AA1"] and [c.ssn for c in cs] == ["SS0", "SS1"] and [c.iin for c in cs] == ["II0", "II1"]
            P.op("act", lambda: ACT.activation(out=R_SS.pair(n), in_=R_AA.pair(n), func=AF.Square),
                 reads=["AA0", "AA1"], writes=["SS0", "SS1"])

        def a_sqrt_pair(cs):
            n = cs[0].B.ntok
            P.op("act", lambda: ACT.activation(out=R_SS.pair(n), in_=R_SS.pair(n), func=AF.Sqrt, scale=-0.25, bias=0.25),
                 reads=["SS0", "SS1"], writes=["SS0", "SS1"])

        def a_bmult_pair(cs):
            n = cs[0].B.ntok
            P.op("dve", lambda: V.tensor_tensor(out=R_II.pair(n), in0=R_II.pair(n), in1=R_SS.pair(n), op=ALU.mult),
                 reads=["II0", "II1", "SS0", "SS1"], writes=["II0", "II1"])

        def a_scan(c):
            B, j = c.B, c.j
            n, ns, ts = B.ntok, B.nseq, B.ts
            AA, II, SS, SG = c.AA, c.II, c.SS, c.SG
            for s in range(ns):
                P.op("dve", (lambda s=s: V.tensor_tensor_scan(
                    out=SS[:, s * ts:(s + 1) * ts], data0=AA[:, s * ts:(s + 1) * ts],
                    data1=II[:, s * ts:(s + 1) * ts], initial=B.hst[:, j, s:s + 1], op0=ALU.mult, op1=ALU.add)),
                    reads=[c.aan, c.iin, B.hst_n + str(j)], writes=[c.ssn])
            yn = "yT%d_%d" % (B.buf, j)
            ywritten.add((B.buf, j))
            P.op("pool", lambda: PO.tensor_tensor(out=yT[B.buf][:, j * NT:j * NT + n], in0=SS[:, 0:n], in1=SG[:, 0:n], op=ALU.mult),
                 reads=[c.ssn, c.sgn], writes=[yn, "yT%d" % B.buf])
            P.op("pool", lambda: PO.tensor_copy(out=B.hst[:, j, :], in_=v3(SS[:, 0:n], ns)[:, :, ts - 1]),
                 reads=[c.ssn], writes=[B.hst_n + str(j), B.hst_n])

        def b1_s1(B, j):
            c = Ctx()
            c.B, c.j = B, j
            n, ns, ts = B.ntok, B.nseq, B.ts
            W = ns * (2 + ts)
            ext, en = R_ext.next()
            enh, enb = en + "h", en + "b"
            CG, cgn = R_CG.next()
            c.SS, c.ssn = R_UB.next()
            SS = c.SS
            e3 = v3(ext[:, 0:W], ns)
            bc, bcn = ip(B, 24 + j)
            bxb, bxbn = ip(B, 16 + j)
            P.op("act", lambda: ACT.activation(out=CG[:, 0:n], in_=bc[:, 0:n], func=AF.Copy), reads=[bcn], writes=[cgn])
            P.op("dve", lambda: V.tensor_copy(out=e3[:, :, 0:2], in_=B.vin[:, j, :, :]), reads=[B.vin_n], writes=[enh])
            P.op("dve", lambda: V.tensor_tensor(out=e3[:, :, 2:2 + ts], in0=v3(bxb[:, 0:n], ns), in1=v3(CG[:, 0:n], ns),
                                                op=ALU.mult), reads=[bxbn, cgn], writes=[enb])
            P.op("dve", lambda: V.tensor_scalar(out=v3(SS[:, 0:n], ns), in0=e3[:, :, 2:2 + ts],
                                                scalar1=prm[:, C_WCB + 8 + j:C_WCB + 9 + j], scalar2=None, op0=ALU.mult),
                 reads=[enb, "prm"], writes=[c.ssn])
            for k in (1, 0):
                P.op("dve", (lambda k=k: V.scalar_tensor_tensor(
                    out=v3(SS[:, 0:n], ns), in0=e3[:, :, k:k + ts], scalar=prm[:, C_WCB + k * 4 + j:C_WCB + k * 4 + j + 1],
                    in1=v3(SS[:, 0:n], ns), op0=ALU.mult, op1=ALU.add)), reads=[enh, enb, c.ssn, "prm"], writes=[c.ssn])
            P.op("pool", lambda: PO.tensor_copy(out=B.vout[:, j, :, :], in_=e3[:, :, ts:ts + 2]), reads=[enb], writes=[B.vout_n])
            return c

        def b2_s1(c):
            B, j, n = c.B, c.j, c.B.ntok
            SS = c.SS
            SG, sgn = R_SGB.next()
            bb, bbn = ip(B, 20 + j)
            bgb, bgbn = ip(B, 28 + j)
            P.op("act", lambda: ACT.activation(out=SG[:, 0:n], in_=bgb[:, 0:n], func=AF.Silu), reads=[bgbn], writes=[sgn])
            P.op("dve", lambda: V.tensor_tensor(out=SS[:, 0:n], in0=bb[:, 0:n], in1=SS[:, 0:n], op=ALU.mult),
                 reads=[bbn, c.ssn], writes=[c.ssn])
            yn = "yT%d_%d" % (B.buf, 8 + j)
            P.op("pool", lambda: PO.tensor_tensor(out=yT[B.buf][:, (8 + j) * NT:(8 + j) * NT + n], in0=SS[:, 0:n], in1=SG[:, 0:n],
                                                  op=ALU.mult), reads=[c.ssn, sgn], writes=[yn, "yT%d" % B.buf])

        def out_load(B, tt):
            sl = xr_next()
            B.oslots = getattr(B, "oslots", {})
            B.oslots[tt] = sl
            if B.is_sample:
                B.tslot = xr_next()
            np_ = B.np_
            r0 = B.row0 + tt * 128
            P.dma("sp", "ld_xr%d" % sl, lambda: nc.sync.dma_start(out=XR[sl][0:np_, :], in_=B.xsrc[r0:r0 + np_, :]),
                  writes=["XR%d" % sl])

        O0 = O

        def out_half(B, tt, hh):
            sl = B.oslots[tt]
            np_ = B.np_
            O, O_n = (O0, "O") if hh == 0 else (T, "T")
            for c in range(12):
                P.op("pe", (lambda c=c: PE.matmul(O[0:np_, :], lhsT=yT[B.buf][:, c * NT + tt * 128:c * NT + tt * 128 + np_],
                                                  rhs=w_out_sb[:, c, hh * 512:(hh + 1) * 512], start=(c == 0), stop=(c == 11))),
                     reads=["yT%d_%d" % (B.buf, c), "w_out"], writes=[O_n])
            if B.is_sample:
                tl = B.tslot
                assert (1, 2 * hh) not in ywritten and (1, 2 * hh + 1) not in ywritten, "gate_bcS clobbered by schedule"
                P.op("dve", lambda: V.tensor_tensor(out=XR[tl][0:np_, hh * 512:(hh + 1) * 512], in0=O[0:np_, :],
                                                    in1=gate_bcS[0:np_, hh * 512:(hh + 1) * 512], op=ALU.mult),
                     reads=[O_n, gate_bcS_n], writes=["XR%d" % tl])
            else:
                P.op("dve", lambda: V.tensor_tensor(out=XR[sl][0:np_, hh * 512:(hh + 1) * 512], in0=O[0:np_, :],
                                                    in1=XR[sl][0:np_, hh * 512:(hh + 1) * 512], op=ALU.add),
                     reads=[O_n, "XR%d" % sl], writes=["XR%d" % sl])

        def out_fin(B, tt):
            sl = B.oslots[tt]
            np_ = B.np_
            col = cnt["col"] % 16
            cnt["col"] += 1
            B.ocols = getattr(B, "ocols", {})
            B.ocols[tt] = col
            xrn = "XR%d" % sl
            r0 = B.row0 + tt * 128
            if B.is_sample:
                tl = B.tslot
                P.op("pool", lambda: PO.tensor_tensor(out=XR[sl][0:np_, :], in0=XR[sl][0:np_, :], in1=XR[tl][0:np_, :], op=ALU.add),
                     reads=[xrn, "XR%d" % tl], writes=[xrn])
            P.op("act", lambda: ACT.activation(out=junk[0:np_, :], in_=XR[sl][0:np_, :], func=AF.Square,
                                               accum_out=stats[0:np_, 0, col:col + 1]),
                 reads=[xrn], writes=["CG0", "ssq%d" % col])
            P.op("dve", lambda: V.tensor_scalar(out=stats[0:np_, 1, col:col + 1], in0=stats[0:np_, 0, col:col + 1],
                                                scalar1=1.0 / D, scalar2=EPS, op0=ALU.mult, op1=ALU.add),
                 reads=["ssq%d" % col], writes=["msq%d" % col])
            P.op("pool", lambda: PO.tensor_tensor(out=stats[0:np_, 2, col:col + 1], in0=stats[0:np_, 1, col:col + 1],
                                                  in1=mhalf[0:np_, col:col + 1], op=ALU.pow),
                 reads=["msq%d" % col, "mhalf"], writes=["rstd%d" % col])

        def out_fin2(B, tt):
            sl = B.oslots[tt]
            np_ = B.np_
            col = B.ocols[tt]
            xrn = "XR%d" % sl
            r0 = B.row0 + tt * 128
            P.op("dve", lambda: V.scalar_tensor_tensor(out=XR[sl][0:np_, :], in0=XR[sl][0:np_, :],
                                                       scalar=stats[0:np_, 2, col:col + 1], in1=gfin_bc[0:np_, :],
                                                       op0=ALU.mult, op1=ALU.mult),
                 reads=[xrn, "rstd%d" % col, "gfin_bc"], writes=[xrn])
            P.dma("sp", "st_y%d" % sl, lambda: nc.sync.dma_start(out=B.ydst[r0:r0 + np_, :], in_=XR[sl][0:np_, :]),
                  reads=[xrn])

        def fold(c):
            k = 1 + (c % 2)
            stg, stg_n = XR[k], "XR%d" % k
            P.dma("sp", "ld_fold_" + stg_n, lambda: nc.sync.dma_start(out=stg[:, :], in_=w_out[c * 128:(c + 1) * 128, :]),
                  writes=[stg_n])
            P.op("dve", lambda: V.tensor_tensor(out=w_out_sb[:, c, :], in0=stg[:, :], in1=gate_bcP[:, :], op=ALU.mult),
                 reads=[stg_n, gate_bcP_n], writes=["w_out"])

        def state_out(src_fn, src_n, nrows, nchunks, dst, stream):
            sl = xr_next()
            for c in range(nchunks):
                P.op("pe", (lambda c=c: PE.transpose(out=T[0:nrows, (c % 4) * 128:(c % 4 + 1) * 128], in_=src_fn(c),
                                                     identity=idf[:, :])), reads=[src_n, "idf"], writes=["T"])
                if c % 4 == 3 or c == nchunks - 1:
                    c0 = (c // 4) * 4
                    w = (c - c0 + 1) * 128
                    P.op("dve", (lambda c0=c0, w=w: V.tensor_copy(out=XR[sl][0:nrows, c0 * 128:c0 * 128 + w], in_=T[0:nrows, 0:w])),
                         reads=["T"], writes=["XR%d" % sl])
            P.dma("sp", stream + "_%d" % sl, lambda: nc.sync.dma_start(out=dst, in_=XR[sl][0:nrows, 0:nchunks * 128]),
                  reads=["XR%d" % sl])

        def run_side(it):
            if it[0] == "T":
                t_group(it[1], it[2])
            elif it[0] == "OL":
                out_load(it[1], it[2])
            elif it[0] == "OH":
                out_half(it[1], it[2], it[3])
            elif it[0] == "OF":
                out_fin(it[1], it[2])
            elif it[0] == "OG":
                out_fin2(it[1], it[2])
            elif it[0] == "FOLD":
                fold(it[1])
            elif it[0] == "SO":
                it[1]()

        LAG = 2
        T_UNITS = (3, 5, 7, 9, 11, 13, 14, 15)
        SIDE_O_START = 3
        nblk = len(blocks)
        in_load(blocks[0])
        in_norm(blocks[0])
        allbanks = [(T, "T"), (O, "O"), (PB[0], "P0"), (PB[1], "P1"), (PB[2], "P2"), (PB[3], "P3"), (GB[0], "G0"), (GB[1], "G1")]
        for kc in range(KC):
            t_group(blocks[0], kc, allbanks[kc])
        if nblk > 1:
            in_load(blocks[1])
        for i, B in enumerate(blocks):
            nxt2 = blocks[i + 2] if i + 2 < nblk else None
            nxt = blocks[i + 1] if i + 1 < nblk else None
            prv = blocks[i - 1] if i >= 1 else None
            units = []
            for j in range(4):
                units += [("A", 2 * j), ("A", 2 * j + 1), ("B1", j), ("B2", j)]
            side_t, side_o = [], []
            if nxt is not None:
                side_t = [("T", nxt, kc) for kc in range(KC)]
            if prv is not None:
                for tt in range(prv.ntt):
                    side_o += [("OL", prv, tt), ("OH", prv, tt, 0), ("OH", prv, tt, 1), ("OF", prv, tt), ("OG", prv, tt)]
                if prv.is_sample:
                    side_o += [("FOLD", c) for c in range(12)]
                    side_o.append(("SO", lambda: state_out(lambda c: hist_s2[:, c, :, :].rearrange("p s k -> p (s k)"),
                                                           "hist_s2", NS * 3, 8, o_cas, "st_s3")))
                    side_o.append(("SO", lambda: state_out(lambda c: hst_s[:, c, :], "hst_s", NS, 8, o_lrs, "st_s4")))
                    side_o.append(("SO", lambda: state_out(lambda c: vhist_s2[:, c, :, :].rearrange("p s k -> p (s k)"),
                                                           "vhist_s2", NS * 2, 4, o_cbs, "st_s5")))
            wait_g, wait_sq, wait_sc, bctx = [], [], [], {}
            for u, (kind, j) in enumerate(units + [("X", 0), ("X", 0)]):
                if wait_sc:
                    a_bmult_pair(wait_sc)
                for c in wait_sc:
                    a_scan(c)
                wait_sc = []
                while wait_g and wait_g[0].u + LAG <= u:
                    c = wait_g.pop(0)
                    a_gates(c)
                    a_s2a(c)
                    wait_sq.append(c)
                if len(wait_sq) >= 2:
                    for c in wait_sq:
                        a_exp(c)
                    a_sq_pair(wait_sq)
                    a_sqrt_pair(wait_sq)
                    wait_sc = wait_sq
                    wait_sq = []
                if kind == "X":
                    continue
                if side_t and u in T_UNITS:
                    run_side(side_t.pop(0))
                    if not side_t and nxt2 is not None:
                        in_load(nxt2)
                        nxt2 = None
                if u >= SIDE_O_START:
                    nside = 2 if len(side_o) > (15 - u) else 1
                    for _ in range(nside):
                        if side_o:
                            run_side(side_o.pop(0))
                if kind == "A":
                    c = a_s1(B, j)
                    c.u = u
                    wait_g.append(c)
                elif kind == "B1":
                    bctx[j] = b1_s1(B, j)
                else:
                    b2_s1(bctx[j])
                if nxt is not None and u == 0:
                    in_norm(nxt)
            assert not wait_g and not wait_sq and not wait_sc
            while side_t:
                run_side(side_t.pop(0))
            if nxt2 is not None:
                in_load(nxt2)
            while side_o:
                run_side(side_o.pop(0))
            if B.name == "p%d" % (NBP - 1):
                hf, vf = hist_p[NBP % 2], vhist_p[NBP % 2]
                state_out(lambda c: hf[:, c, 0, :], "hist_p%d" % (NBP % 2), 3, 8, o_cap, "st_s0")
                state_out(lambda c: hst_p[:, c, :], "hst_p", 1, 8, o_lrp, "st_s1")
                state_out(lambda c: vf[:, c, 0, :], "vhist_p%d" % (NBP % 2), 2, 4, o_cbp, "st_s2")
        B = blocks[-1]
        for tt in range(B.ntt):
            out_load(B, tt)
            out_half(B, tt, 0)
            out_half(B, tt, 1)
            out_fin(B, tt)
            out_fin2(B, tt)

        finals = [s for s in P.streams if s.startswith("st_")]
        P.emit(final_streams=finals)
    return nc


_NC_CACHE = {}


def _get_nc(TP):
    if TP not in _NC_CACHE:
        _NC_CACHE[TP] = build_nc(TP)
    return _NC_CACHE[TP]


def _f32(a):
    return np.ascontiguousarray(np.asarray(a, dtype=np.float32))


def kernel(x_prompt, x_sample, c_prompt, c_sample, cache_conv_a, state_lru, cache_conv_b,
           g_norm, w_ada, b_ada, w_in, w_conv_a, b_conv_a, w_rg, b_rg, w_ig, b_ig, lam,
           w_conv_b, w_out, g_final, _TP=None):
    x_prompt = _f32(x_prompt)
    TP = x_prompt.shape[1] if _TP is None else _TP
    nc = _get_nc(TP)
    x_sample, c_prompt, c_sample = _f32(x_sample), _f32(c_prompt), _f32(c_sample)
    cache_conv_a, state_lru, cache_conv_b = _f32(cache_conv_a), _f32(state_lru), _f32(cache_conv_b)
    pv = np.concatenate([
        _f32(g_norm).reshape(8, 128), _f32(w_conv_a).reshape(32, 128), _f32(b_conv_a).reshape(8, 128),
        _f32(b_rg).reshape(8, 128), _f32(b_ig).reshape(8, 128), _f32(lam).reshape(8, 128),
        _f32(w_conv_b).reshape(12, 128), _f32(b_ada).reshape(24, 128)], axis=0)
    assert pv.shape == (NPV, 128)

    def blockdiag(w):
        w = _f32(w).reshape(16, 64, 64)
        o = np.zeros((128, 8, 128), np.float32)
        for c in range(8):
            o[0:64, c, 0:64] = w[2 * c]
            o[64:128, c, 64:128] = w[2 * c + 1]
        return o

    wrg_bd, wig_bd = blockdiag(w_rg), blockdiag(w_ig)
    shared = {
        "pv": np.ascontiguousarray(pv), "b_ada": _f32(b_ada).reshape(1, 3 * D), "g_fin": _f32(g_final).reshape(1, D),
        "w_ada": _f32(w_ada).reshape(D, 3 * D), "w_in": np.ascontiguousarray(_f32(w_in).reshape(D, 32, 128)[:, W_ORDER, :].reshape(D, 4096)), "w_out": _f32(w_out).reshape(1536, D),
        "wrg": wrg_bd, "wig": wig_bd,
    }
    in_maps = []
    for i in range(NCORES):
        m = dict(shared)
        m["xp"] = np.ascontiguousarray(x_prompt[i, :TP])
        m["xs"] = np.ascontiguousarray(x_sample[NS * i:NS * (i + 1)].reshape(NS * TS, D))
        m["c5"] = np.ascontiguousarray(np.concatenate([c_prompt[i:i + 1], c_sample[NS * i:NS * (i + 1)]], axis=0))
        m["cca"] = np.ascontiguousarray(cache_conv_a[0, NS * i:NS * (i + 1)].reshape(NS * 3, D))
        m["slru"] = np.ascontiguousarray(state_lru[0, NS * i:NS * (i + 1)])
        m["ccb"] = np.ascontiguousarray(cache_conv_b[0, NS * i:NS * (i + 1)].reshape(NS * 2, 512))
        in_maps.append(m)
    res = run_bass_kernel_spmd(nc, in_maps, core_ids=list(range(NCORES)))
    R = res.results
    y_prompt = np.stack([R[i]["yp"] for i in range(NCORES)], axis=0)
    y_sample = np.concatenate([R[i]["ys"].reshape(NS, TS, D) for i in range(NCORES)], axis=0)
    conv_a_prompt = np.stack([R[i]["o_cap"] for i in range(NCORES)], axis=0)[None]
    lru_prompt = np.stack([R[i]["o_lrp"].reshape(D) for i in range(NCORES)], axis=0)[None]
    conv_b_prompt = np.stack([R[i]["o_cbp"] for i in range(NCORES)], axis=0)[None]
    conv_a_sample = np.concatenate([R[i]["o_cas"].reshape(NS, 3, D) for i in range(NCORES)], axis=0)[None]
    lru_sample = np.concatenate([R[i]["o_lrs"] for i in range(NCORES)], axis=0)[None]
    conv_b_sample = np.concatenate([R[i]["o_cbs"].reshape(NS, 2, 512) for i in range(NCORES)], axis=0)[None]
    outs = (y_prompt, y_sample, conv_a_prompt, lru_prompt, conv_b_prompt, conv_a_sample, lru_sample, conv_b_sample)
    return tuple(np.ascontiguousarray(o.astype(np.float32)) for o in outs)
```

```python
import contextlib
import numpy as np
import concourse.bass as bass
import concourse.mybir as mybir
from concourse.bass_utils import run_bass_kernel_spmd

F32 = mybir.dt.float32
BF16 = mybir.dt.bfloat16
AF = mybir.ActivationFunctionType
ALU = mybir.AluOpType

D = 1024
KC = 8
NT = 512
EPS = 1e-6
NS = 4
TS = 16
NCORES = 8


class _Op:
    __slots__ = ("eng", "idx", "fn", "waits", "kn", "has_dep", "stream", "semval")


class Prog:
    CENG = ("pe", "act", "dve", "pool", "sp")

    def __init__(self, nc):
        self.nc = nc
        self.ops = {e: [] for e in self.CENG}
        self.streams = {}
        self.ccount = {}
        self.last_w = {}
        self.readers = {}
        self.known = {e: {} for e in self.CENG}
        self.opmap = {}
        self.eng_obj = {"pe": nc.tensor, "act": nc.scalar, "dve": nc.vector, "pool": nc.gpsimd, "sp": nc.sync}

    def _add(self, eng, fn, reads, writes, stream=None):
        op = _Op()
        op.eng = eng
        op.fn = fn
        op.stream = stream
        op.has_dep = False
        op.semval = None
        if stream is None:
            counter = eng
            n = self.ccount.get(eng, 0)
            self.ccount[eng] = n + 1
        else:
            counter = "dma:" + stream
            n = self.streams.get(stream, 0)
            self.streams[stream] = n + 1
        key = (counter, n)
        op.idx = key
        deps = set()
        for r in reads:
            w = self.last_w.get(r)
            if w is not None:
                deps.add(w)
        for r in writes:
            w = self.last_w.get(r)
            if w is not None:
                deps.add(w)
            for rd in self.readers.get(r, ()):
                deps.add(rd)
        kn = self.known[eng]
        best = {}
        deps = {((c, self.streams[c[4:]] - 1) if c.startswith("dma:") else (c, m)) for c, m in deps}
        for c, m in deps:
            if (c, m) == key:
                continue
            if c == "pe" and eng == "pe" and stream is None:
                continue
            if kn.get(c, -1) >= m:
                continue
            if best.get(c, -1) < m:
                best[c] = m
        op.waits = sorted(best.items())
        for c, m in op.waits:
            src = self.opmap[(c, m)]
            src.has_dep = True
            for c2, m2 in src.kn.items():
                if kn.get(c2, -1) < m2:
                    kn[c2] = m2
            if kn.get(c, -1) < m:
                kn[c] = m
        op.kn = dict(kn)
        self.opmap[key] = op
        self.ops[eng].append(op)
        for r in reads:
            self.readers.setdefault(r, []).append(key)
        for r in writes:
            self.last_w[r] = key
            self.readers[r] = []
        return op

    def op(self, eng, fn, reads=(), writes=()):
        return self._add(eng, fn, tuple(reads), tuple(writes), None)

    def dma(self, q, stream, fn, reads=(), writes=()):
        return self._add(q, fn, tuple(reads), tuple(writes), stream)

    def emit(self, final_streams=()):
        nc = self.nc
        with contextlib.ExitStack() as es:
            sems = {}
            for e in self.CENG:
                sems[e] = es.enter_context(nc.semaphore("s_" + e))
            for s in self.streams:
                sems["dma:" + s] = es.enter_context(nc.semaphore("d_" + s))
            for e in self.CENG:
                cnt = 0
                for o in self.ops[e]:
                    if o.stream is None:
                        if o.has_dep:
                            cnt += 1
                            o.semval = cnt
                    else:
                        o.semval = 16 * (o.idx[1] + 1)
            block = es.enter_context(nc.Block())

            def body(e):
                eng = self.eng_obj[e]
                for o in self.ops[e]:
                    for c, m in o.waits:
                        eng.wait_ge(sems[c], self.opmap[(c, m)].semval)
                    inst = o.fn()
                    if o.stream is not None:
                        inst.then_inc(sems["dma:" + o.stream], 16)
                    elif o.has_dep:
                        inst.then_inc(sems[e], 1)
                if e == "sp":
                    for s in final_streams:
                        n = self.streams.get(s, 0)
                        if n:
                            eng.wait_ge(sems["dma:" + s], 16 * n)

            @block.tensor
            def _(t):
                body("pe")

            @block.scalar
            def _(t):
                body("act")

            @block.vector
            def _(t):
                body("dve")

            @block.gpsimd
            def _(t):
                body("pool")

            @block.sync
            def _(t):
                body("sp")


class _Blk:
    pass


C_GN, C_WCA, C_BCA, C_BRG, C_BIG, C_LAM, C_WCB, C_BADA = 0, 8, 40, 48, 56, 64, 72, 84
NPV = 108
W_ORDER = []
for _j in range(4):
    W_ORDER += [2 * _j, 8 + 2 * _j, 2 * _j + 1, 9 + 2 * _j, 24 + _j, 16 + _j, 20 + _j, 28 + _j]
W_POS = {pc: i for i, pc in enumerate(W_ORDER)}


def build_nc(TP):
    NBP = TP // NT
    nc = bass.Bass("TRN2", target_bir_lowering=False)

    def din(name, shape):
        return nc.dram_tensor(name, shape, F32, kind="ExternalInput").ap()

    def dout(name, shape):
        return nc.dram_tensor(name, shape, F32, kind="ExternalOutput").ap()

    xp = din("xp", [TP, D])
    xs = din("xs", [NS * TS, D])
    c5 = din("c5", [5, D])
    cca = din("cca", [NS * 3, D])
    slru = din("slru", [NS, D])
    ccb = din("ccb", [NS * 2, 512])
    pv = din("pv", [NPV, 128])
    b_ada = din("b_ada", [1, 3 * D])
    g_fin = din("g_fin", [1, D])
    w_ada = din("w_ada", [D, 3 * D])
    w_in = din("w_in", [D, 4096])
    w_out = din("w_out", [1536, D])
    wrg = din("wrg", [128, 8, 128])
    wig = din("wig", [128, 8, 128])

    yp = dout("yp", [TP, D])
    ys = dout("ys", [NS * TS, D])
    o_cap = dout("o_cap", [3, D])
    o_lrp = dout("o_lrp", [1, D])
    o_cbp = dout("o_cbp", [2, 512])
    o_cas = dout("o_cas", [NS * 3, D])
    o_lrs = dout("o_lrs", [NS, D])
    o_cbs = dout("o_cbs", [NS * 2, 512])

    with contextlib.ExitStack() as es:
        def sb(name, shape, dt):
            return es.enter_context(nc.sbuf_tensor(name, shape, dt))

        def ps(name):
            return es.enter_context(nc.psum_tensor(name, [128, 512], F32))

        w_in_sb = sb("w_in_sb", [128, KC, 4096], BF16)
        w_out_sb = sb("w_out_sb", [128, 12, D], BF16)
        wg_sb = sb("wg_sb", [128, 2, 8, 128], BF16)
        idf = sb("idf", [128, 128], F32)
        prm = sb("prm", [128, NPV], F32)
        cst = sb("cst", [128, 40], F32)
        scl = sb("scl", [128, KC, 5], F32)
        shf = sb("shf", [128, KC, 5], F32)
        scT = sb("scT", [128, KC * 5], F32)
        gfin_bc = sb("gfin_bc", [128, D], F32)
        hist_p = [sb("hist_p%d" % i, [128, 8, 1, 3], F32) for i in range(2)]
        hst_p = sb("hst_p", [128, 8, 1], F32)
        vhist_p = [sb("vhist_p%d" % i, [128, 4, 1, 2], F32) for i in range(2)]
        hist_s = sb("hist_s", [128, 8, NS, 3], F32)
        hist_s2 = sb("hist_s2", [128, 8, NS, 3], F32)
        hst_s = sb("hst_s", [128, 8, NS], F32)
        vhist_s = sb("vhist_s", [128, 4, NS, 2], F32)
        vhist_s2 = sb("vhist_s2", [128, 4, NS, 2], F32)
        stats = sb("stats", [128, 4, 16], F32)
        mhalf = sb("mhalf", [128, 16], F32)
        small = sb("small", [128, 256], F32)

        NXIN = 4
        x_in = [sb("x_in%d" % i, [128, D], F32) for i in range(NXIN)]
        hnT = [sb("hnT%d" % i, [128, KC * NT], BF16) for i in range(2)]
        yT = [sb("yT%d" % i, [128, 12 * NT], BF16) for i in range(2)]
        NXR = 4
        XR = [sb("XR%d" % i, [128, D], F32) for i in range(NXR)]
        EXTW = 3 + NT

        class Rot:
            def __init__(self, name, n, width, dt):
                self.tiles = [sb("%s%d" % (name, i), [128, width], dt) for i in range(n)]
                self.name, self.i = name, 0

            def next(self):
                k = self.i % len(self.tiles)
                self.i += 1
                return self.tiles[k], "%s%d" % (self.name, k)

        R_ext = Rot("ext", 2, EXTW, F32)
        R_XC = Rot("XC", 2, NT, F32)
        R_XB = Rot("XB", 2, NT, BF16)
        class RotPair(Rot):
            def __init__(self, name):
                self.base = sb(name + "p", [128, 2 * NT], F32)
                self.tiles = [self.base[:, 0:NT], self.base[:, NT:2 * NT]]
                self.name, self.i = name, 0

            def pair(self, n):
                return self.base[:].rearrange("p (k t) -> p k t", k=2)[:, :, 0:n]

        R_AA = RotPair("AA")
        R_CG = Rot("CG", 1, NT, F32)
        R_II = RotPair("II")
        R_SS = RotPair("SS")
        R_UB = Rot("UB", 1, NT, F32)
        R_SG = Rot("SG", 3, NT, F32)
        R_SGB = Rot("SGB", 1, NT, F32)
        junk = R_CG.tiles[0][:].bitcast(BF16)

        T = ps("T")
        O = ps("O")
        PB = [ps("P%d" % i) for i in range(4)]
        GB = [ps("G%d" % i) for i in range(2)]

        P = Prog(nc)
        V, ACT, PE, PO = nc.vector, nc.scalar, nc.tensor, nc.gpsimd

        stg_tiles = [(XR[3], "XR3"), (XR[0], "XR0"), (XR[1], "XR1"), (XR[2], "XR2")]
        c5_sb, c5_n = x_in[0], "x_in0"
        bada_sb, bada_n = [x_in[1], x_in[2], x_in[3]], ["x_in1", "x_in2", "x_in3"]
        gate_bcP, gate_bcP_n = XR[3], "XR3"
        yT1f = yT[1][:].bitcast(F32)
        gate_bcS, gate_bcS_n = yT1f[:, 0:D], "yT1"
        st2_sb = yT1f[:, 2048:2560]

        ones_row = small[0:1, 0:128]
        selP = small[0:5, 0:128]
        selS = small[0:5, 128:192]
        P.op("pool", lambda: PO.memset(idf[:], 0.0), writes=["idf"])
        P.op("pool", lambda: PO.affine_select(out=idf[:], in_=idf[:], pattern=[[-1, 128]], compare_op=ALU.not_equal,
                                              fill=1.0, base=0, channel_multiplier=1), reads=["idf"], writes=["idf"])
        P.op("pool", lambda: PO.memset(small[:], 1.0), writes=["small"])
        P.op("pool", lambda: PO.affine_select(out=small[0:5, 0:128], in_=small[0:5, 0:128], pattern=[[0, 128]],
                                              compare_op=ALU.is_ge, fill=0.0, base=0, channel_multiplier=-1),
             reads=["small"], writes=["small"])
        P.op("pool", lambda: PO.affine_select(out=small[0:5, 128:192], in_=small[0:5, 128:192], pattern=[[1, 64]],
                                              compare_op=ALU.is_ge, fill=0.0, base=16, channel_multiplier=-16),
             reads=["small"], writes=["small"])
        P.op("pool", lambda: PO.affine_select(out=small[0:5, 128:192], in_=small[0:5, 128:192], pattern=[[-1, 64]],
                                              compare_op=ALU.is_ge, fill=0.0, base=-1, channel_multiplier=16),
             reads=["small"], writes=["small"])
        P.op("pool", lambda: PO.memset(mhalf[:], -0.5), writes=["mhalf"])
        P.op("pool", lambda: PO.memset(hist_p[0][:], 0.0), writes=["hist_p0"])
        P.op("pool", lambda: PO.memset(hst_p[:], 0.0), writes=["hst_p"])
        P.op("pool", lambda: PO.memset(vhist_p[0][:], 0.0), writes=["vhist_p0"])

        pv_sb, pv_n = hnT[1][:].bitcast(F32), "hnT1"
        P.dma("sp", "ld_pv", lambda: nc.sync.dma_start(out=pv_sb[0:NPV, 0:128], in_=pv), writes=[pv_n])
        P.dma("sp", "ld_c5", lambda: nc.sync.dma_start(out=c5_sb[0:5, :], in_=c5), writes=[c5_n])
        for s in range(5):
            for t3 in range(3):
                P.dma("sp", "ld_bada", (lambda s=s, t3=t3: nc.sync.dma_start(
                    out=bada_sb[t3][s:s + 1, :], in_=b_ada[0:1, t3 * D:(t3 + 1) * D])), writes=[bada_n[t3]])
        gf_sb, gf_n = hnT[1][:].bitcast(F32), "hnT1"
        P.dma("sp", "ld_gf", lambda: nc.sync.dma_start(out=gf_sb[0:1, 1024:2048], in_=g_fin), writes=[gf_n])
        st_sb = hnT[0][:].bitcast(F32)
        P.dma("sp", "ld_st", lambda: nc.sync.dma_start(out=st_sb[0:NS * 3, 0:1024], in_=cca), writes=["hnT0"])
        P.dma("sp", "ld_st", lambda: nc.sync.dma_start(out=st_sb[0:NS, 1024:2048], in_=slru), writes=["hnT0"])
        P.dma("sp", "ld_st", lambda: nc.sync.dma_start(out=st2_sb[0:NS * 2, 0:512], in_=ccb), writes=["yT1"])

        P.dma("pool", "ld_wg", lambda: PO.dma_start(out=wg_sb[:, 0, :, :], in_=wrg), writes=["wg"])
        P.dma("pool", "ld_wg", lambda: PO.dma_start(out=wg_sb[:, 1, :, :], in_=wig), writes=["wg"])

        P.op("pe", lambda: PE.transpose(out=T[:, 0:NPV], in_=pv_sb[0:NPV, 0:128], identity=idf[0:NPV, 0:NPV]),
             reads=[pv_n, "idf"], writes=["T"])
        P.op("dve", lambda: V.tensor_copy(out=prm[:], in_=T[:, 0:NPV]), reads=["T"], writes=["prm"])
        P.op("dve", lambda: V.tensor_scalar(out=cst[:, 0:16], in0=prm[:, C_BRG:C_BRG + 16], scalar1=0.5, scalar2=None,
                                            op0=ALU.mult), reads=["prm"], writes=["cst_hb"])
        P.op("act", lambda: ACT.activation(out=cst[:, 24:32], in_=prm[:, C_LAM:C_LAM + 8], func=AF.Exp, scale=-1.0),
             reads=["prm"], writes=["cst_t"])
        P.op("act", lambda: ACT.activation(out=cst[:, 32:40], in_=cst[:, 24:32], func=AF.Ln, bias=1.0),
             reads=["cst_t"], writes=["cst_t2"])
        P.op("dve", lambda: V.tensor_scalar(out=cst[:, 16:24], in0=cst[:, 32:40], scalar1=-4.0, scalar2=None,
                                            op0=ALU.mult), reads=["cst_t2"], writes=["cst_hc"])

        for kc in range(KC):
            P.op("pe", (lambda kc=kc: PE.transpose(out=T[:, 128 + kc * 5:128 + (kc + 1) * 5],
                                                   in_=c5_sb[0:5, kc * 128:(kc + 1) * 128], identity=idf[0:5, 0:5])),
                 reads=[c5_n, "idf"], writes=["T"])
        P.op("act", lambda: ACT.activation(out=scT[:], in_=T[:, 128:168], func=AF.Silu), reads=["T"], writes=["scT"])

        abanks = [(PB[0], "P0"), (PB[1], "P1"), (PB[2], "P2"), (PB[3], "P3"), (GB[0], "G0"), (GB[1], "G1")]
        spool = [(t, n, D) for t, n in stg_tiles]
        for R_ in (R_XC, R_AA, R_II, R_SS, R_SG):
            spool += [(t, "%s%d" % (R_.name, i), NT) for i, t in enumerate(R_.tiles)]
        piece = 0
        for kc in range(KC):
            col = 0
            while col < 3 * D:
                stg, stg_n, cap = spool[piece % len(spool)]
                piece += 1
                wdt = min(cap, 3 * D - col)
                P.dma("sp", "ld_wada_" + stg_n, (lambda kc=kc, col=col, wdt=wdt, stg=stg: nc.sync.dma_start(
                    out=stg[:, 0:wdt], in_=w_ada[kc * 128:(kc + 1) * 128, col:col + wdt])), writes=[stg_n])
                for hh in range(wdt // 512):
                    n6 = (col + hh * 512) // 512
                    bk, bk_n = abanks[n6]
                    last = (col + (hh + 1) * 512 == 3 * D)
                    tok = ["tok_win"] if (kc == 7 and last) else []
                    P.op("pe", (lambda kc=kc, hh=hh, stg=stg, bk=bk: PE.matmul(
                        bk[0:5, :], lhsT=scT[:, kc * 5:(kc + 1) * 5], rhs=stg[:, hh * 512:(hh + 1) * 512],
                        start=(kc == 0), stop=(kc == KC - 1))), reads=["scT", stg_n], writes=[bk_n] + tok)
                col += wdt
            if kc == 7:
                w_in3 = w_in.rearrange("(kc p) n -> p kc n", p=128)
                for g in range(8):
                    P.dma("pool", "ld_win%d" % g, (lambda g=g: PO.dma_start(
                        out=w_in_sb[:, :, g * 512:(g + 1) * 512], in_=w_in3[:, :, g * 512:(g + 1) * 512])),
                        reads=["tok_win"], writes=["w_in_g%d" % g])
        for c in range(12):
            P.dma("pool", "ld_wout", (lambda c=c: PO.dma_start(out=w_out_sb[:, c, :], in_=w_out[c * 128:(c + 1) * 128, :])),
                  writes=["w_out"])

        ada_sb, ada_n = yT[0][:].bitcast(F32), "yT0"
        for n6 in range(6):
            bk, bk_n = abanks[n6]
            t3, hh = divmod(n6, 2)
            P.op("dve", (lambda n6=n6, bk=bk, t3=t3, hh=hh: V.tensor_tensor(
                out=ada_sb[0:5, n6 * 512:(n6 + 1) * 512], in0=bk[0:5, :],
                in1=bada_sb[t3][0:5, hh * 512:(hh + 1) * 512], op=ALU.add)),
                reads=[bk_n, bada_n[t3]], writes=[ada_n])
        for j in range(16):
            P.op("pe", (lambda j=j: PE.transpose(out=T[:, 256 + j * 5:256 + (j + 1) * 5],
                                                 in_=ada_sb[0:5, j * 128:(j + 1) * 128], identity=idf[0:5, 0:5])),
                 reads=[ada_n, "idf"], writes=["T"])
        P.op("dve", lambda: V.tensor_copy(out=shf[:].rearrange("p k s -> p (k s)"), in_=T[:, 256:296]),
             reads=["T"], writes=["shf"])
        for kc in range(KC):
            P.op("dve", (lambda kc=kc: V.tensor_scalar(out=scl[:, kc, :], in0=T[:, 296 + kc * 5:296 + (kc + 1) * 5],
                                                       scalar1=1.0, scalar2=prm[:, C_GN + kc:C_GN + kc + 1],
                                                       op0=ALU.add, op1=ALU.mult)),
                 reads=["T", "prm"], writes=["scl"])
        gbanks = [(O, "O"), (PB[0], "P0"), (PB[1], "P1"), (PB[2], "P2"), (PB[3], "P3"), (GB[0], "G0")]
        for hh in range(2):
            (b0, b0n), (b1, b1n), (b2, b2n) = gbanks[3 * hh:3 * hh + 3]
            P.op("pe", (lambda hh=hh, b0=b0: PE.matmul(b0[:, :], lhsT=selP, rhs=ada_sb[0:5, 2048 + hh * 512:2048 + (hh + 1) * 512],
                                                       start=True, stop=True)), reads=["small", ada_n], writes=[b0n])
            P.op("pe", (lambda hh=hh, b1=b1: PE.matmul(b1[0:64, :], lhsT=selS, rhs=ada_sb[0:5, 2048 + hh * 512:2048 + (hh + 1) * 512],
                                                       start=True, stop=True)), reads=["small", ada_n], writes=[b1n])
            P.op("pe", (lambda hh=hh, b2=b2: PE.matmul(b2[:, :], lhsT=ones_row, rhs=gf_sb[0:1, 1024 + hh * 512:1024 + (hh + 1) * 512],
                                                       start=True, stop=True)), reads=["small", gf_n], writes=[b2n])
            P.op("dve", (lambda hh=hh, b0=b0: V.tensor_copy(out=gate_bcP[:, hh * 512:(hh + 1) * 512], in_=b0[:, :])),
                 reads=[b0n], writes=[gate_bcP_n])
            P.op("dve", (lambda hh=hh, b1=b1: V.tensor_copy(out=gate_bcS[0:64, hh * 512:(hh + 1) * 512], in_=b1[0:64, :])),
                 reads=[b1n], writes=[gate_bcS_n])
            P.op("dve", (lambda hh=hh, b2=b2: V.tensor_copy(out=gfin_bc[:, hh * 512:(hh + 1) * 512], in_=b2[:, :])),
                 reads=[b2n], writes=["gfin_bc"])
        for c in range(8):
            P.op("pe", (lambda c=c: PE.transpose(out=T[:, c * 12:(c + 1) * 12], in_=st_sb[0:12, c * 128:(c + 1) * 128],
                                                 identity=idf[0:12, 0:12])), reads=["hnT0", "idf"], writes=["T"])
        P.op("dve", lambda: V.tensor_copy(out=hist_s[:].rearrange("p c s k -> p (c s k)"), in_=T[:, 0:96]),
             reads=["T"], writes=["hist_s"])
        for c in range(8):
            P.op("pe", (lambda c=c: PE.transpose(out=T[:, 96 + c * 4:96 + (c + 1) * 4],
                                                 in_=st_sb[0:4, 1024 + c * 128:1024 + (c + 1) * 128],
                                                 identity=idf[0:4, 0:4])), reads=["hnT0", "idf"], writes=["T"])
        P.op("dve", lambda: V.tensor_copy(out=hst_s[:].rearrange("p c s -> p (c s)"), in_=T[:, 96:128]),
             reads=["T"], writes=["hst_s"])
        for c in range(4):
            P.op("pe", (lambda c=c: PE.transpose(out=T[:, 128 + c * 8:128 + (c + 1) * 8],
                                                 in_=st2_sb[0:8, c * 128:(c + 1) * 128],
                                                 identity=idf[0:8, 0:8])), reads=["yT1", "idf"], writes=["T"])
        P.op("dve", lambda: V.tensor_copy(out=vhist_s[:].rearrange("p c s k -> p (c s k)"), in_=T[:, 128:160]),
             reads=["T"], writes=["vhist_s"])

        blocks = []
        B = _Blk()
        B.name, B.is_sample = "s", True
        B.ntok, B.nseq, B.ts, B.np_, B.ntt = NS * TS, NS, TS, NS * TS, 1
        B.xsrc, B.ydst, B.row0 = xs, ys, 0
        B.seq0 = 1
        B.hin, B.hin_n, B.hout, B.hout_n = hist_s, "hist_s", hist_s2, "hist_s2"
        B.vin, B.vin_n, B.vout, B.vout_n = vhist_s, "vhist_s", vhist_s2, "vhist_s2"
        B.hst, B.hst_n = hst_s, "hst_s"
        blocks.append(B)
        for b in range(NBP):
            B = _Blk()
            B.name, B.is_sample = "p%d" % b, False
            B.ntok, B.nseq, B.ts, B.np_, B.ntt = NT, 1, NT, 128, NT // 128
            B.xsrc, B.ydst, B.row0 = xp, yp, b * NT
            B.seq0 = 0
            B.hin, B.hin_n, B.hout, B.hout_n = hist_p[b % 2], "hist_p%d" % (b % 2), hist_p[(b + 1) % 2], "hist_p%d" % ((b + 1) % 2)
            B.vin, B.vin_n, B.vout, B.vout_n = vhist_p[b % 2], "vhist_p%d" % (b % 2), vhist_p[(b + 1) % 2], "vhist_p%d" % ((b + 1) % 2)
            B.hst, B.hst_n = hst_p, "hst_p"
            blocks.append(B)
        for i, B in enumerate(blocks):
            B.buf = i % 2

        cnt = {"xin": 0, "col": 0, "pb": 0, "xr": 0}
        ywritten = set()

        def v3(ap, nseq):
            return ap.rearrange("p (s t) -> p s t", s=nseq)

        def xr_next():
            sl = cnt["xr"] % NXR
            cnt["xr"] += 1
            return sl

        def in_load(B):
            B.xslots = []
            for tt in range(B.ntt):
                sl = cnt["xin"] % NXIN
                cnt["xin"] += 1
                B.xslots.append(sl)
                np_ = B.np_
                xn = "x_in%d" % sl
                r0 = B.row0 + tt * 128
                P.dma("sp", "ld_x%d" % sl, (lambda sl=sl, r0=r0, np_=np_, B=B: nc.sync.dma_start(
                    out=x_in[sl][0:np_, :], in_=B.xsrc[r0:r0 + np_, :])), writes=[xn])

        def in_norm(B):
            for tt in range(B.ntt):
                sl = B.xslots[tt]
                col = cnt["col"] % 16
                cnt["col"] += 1
                np_ = B.np_
                xn = "x_in%d" % sl
                P.op("act", (lambda sl=sl, np_=np_, col=col: ACT.activation(
                    out=junk[0:np_, :], in_=x_in[sl][0:np_, :], func=AF.Square, accum_out=stats[0:np_, 0, col:col + 1])),
                    reads=[xn], writes=["CG0", "ssq%d" % col])
                P.op("dve", (lambda np_=np_, col=col: V.tensor_scalar(
                    out=stats[0:np_, 1, col:col + 1], in0=stats[0:np_, 0, col:col + 1], scalar1=1.0 / D, scalar2=EPS,
                    op0=ALU.mult, op1=ALU.add)), reads=["ssq%d" % col], writes=["msq%d" % col])
                P.op("pool", (lambda np_=np_, col=col: PO.tensor_tensor(
                    out=stats[0:np_, 2, col:col + 1], in0=stats[0:np_, 1, col:col + 1], in1=mhalf[0:np_, col:col + 1],
                    op=ALU.pow)), reads=["msq%d" % col, "mhalf"], writes=["rstd%d" % col])
                P.op("pool", (lambda sl=sl, np_=np_, col=col: PO.tensor_scalar(
                    out=x_in[sl][0:np_, :], in0=x_in[sl][0:np_, :], scalar1=stats[0:np_, 2, col:col + 1], scalar2=0.0,
                    op0=ALU.mult, op1=ALU.add)), reads=[xn, "rstd%d" % col], writes=[xn])

        T0 = T

        def t_group(B, kc, bank=None):
            T, T_n = bank if bank is not None else (T0, "T")
            hn = "hnT%d_%d" % (B.buf, kc)
            for tt in range(B.ntt):
                sl = B.xslots[tt]
                np_ = B.np_
                P.op("pe", (lambda sl=sl, np_=np_, tt=tt, kc=kc: PE.transpose(
                    out=T[:, tt * 128:tt * 128 + np_], in_=x_in[sl][0:np_, kc * 128:(kc + 1) * 128],
                    identity=idf[0:np_, 0:np_])), reads=["x_in%d" % sl, "idf"], writes=[T_n])
            for s in range(B.nseq):
                q = B.seq0 + s
                P.op("act", (lambda kc=kc, s=s, q=q, B=B: ACT.activation(
                    out=hnT[B.buf][:, kc * NT + s * B.ts:kc * NT + (s + 1) * B.ts], in_=T[:, s * B.ts:(s + 1) * B.ts],
                    func=AF.Identity, scale=scl[:, kc, q:q + 1], bias=shf[:, kc, q:q + 1])),
                    reads=[T_n, "scl", "shf"], writes=[hn, "hnT%d" % B.buf])

        def ip(B, pchunk):
            k = cnt["pb"] % 4
            cnt["pb"] += 1
            bk, bk_n = PB[k], "P%d" % k
            pos = W_POS[pchunk]
            for kc in range(KC):
                P.op("pe", (lambda kc=kc, bk=bk, B=B, pos=pos: PE.matmul(
                    bk[:, 0:B.ntok], lhsT=w_in_sb[:, kc, pos * 128:(pos + 1) * 128],
                    rhs=hnT[B.buf][:, kc * NT:kc * NT + B.ntok], start=(kc == 0), stop=(kc == KC - 1))),
                    reads=["w_in_g%d" % (pos // 4), "hnT%d_%d" % (B.buf, kc)], writes=[bk_n])
            return bk, bk_n

        class Ctx:
            pass

        def a_s1(B, j):
            c = Ctx()
            c.B, c.j = B, j
            n, ns, ts = B.ntok, B.nseq, B.ts
            W = ns * (3 + ts)
            ext, en = R_ext.next()
            c.XC, c.xcn = R_XC.next()
            c.XB, c.xbn = R_XB.next()
            c.SG, c.sgn = R_SG.next()
            XC, XB, SG = c.XC, c.XB, c.SG
            e3 = v3(ext[:, 0:W], ns)
            enh, enb = en + "h", en + "b"
            bx, bxn = ip(B, j)
            bg, bgn = ip(B, 8 + j)
            bx3 = v3(bx[:, 0:n], ns)
            P.op("act", lambda: ACT.activation(out=e3[:, :, 3:3 + ts], in_=bx3, func=AF.Copy), reads=[bxn], writes=[enb])
            P.op("act", lambda: ACT.activation(out=XC[:, 0:n], in_=bx[:, 0:n], func=AF.Identity,
                                               scale=prm[:, C_WCA + 24 + j:C_WCA + 25 + j],
                                               bias=prm[:, C_BCA + j:C_BCA + j + 1]),
                 reads=[bxn, "prm"], writes=[c.xcn])
            P.op("act", lambda: ACT.activation(out=B.hout[:, j, :, :], in_=bx3[:, :, ts - 3:ts], func=AF.Copy),
                 reads=[bxn], writes=[B.hout_n])
            P.op("act", lambda: ACT.activation(out=SG[:, 0:n], in_=bg[:, 0:n], func=AF.Silu), reads=[bgn], writes=[c.sgn])
            P.op("dve", lambda: V.tensor_copy(out=e3[:, :, 0:3], in_=B.hin[:, j, :, :]), reads=[B.hin_n], writes=[enh])
            for k in (2, 1, 0):
                P.op("dve", (lambda k=k: V.scalar_tensor_tensor(
                    out=v3(XC[:, 0:n], ns), in0=e3[:, :, k:k + ts], scalar=prm[:, C_WCA + k * 8 + j:C_WCA + k * 8 + j + 1],
                    in1=v3(XC[:, 0:n], ns), op0=ALU.mult, op1=ALU.add)), reads=[enh, enb, c.xcn, "prm"], writes=[c.xcn])
            P.op("dve", lambda: V.tensor_copy(out=XB[:, 0:n], in_=XC[:, 0:n]), reads=[c.xcn], writes=[c.xbn])
            return c

        def a_gates(c):
            j, n = c.j, c.B.ntok
            XB = c.XB
            P.op("pe", lambda: PE.matmul(GB[0][:, 0:n], lhsT=wg_sb[:, 0, j, :], rhs=XB[:, 0:n], start=True, stop=True),
                 reads=["wg", c.xbn], writes=["G0"])
            P.op("pe", lambda: PE.matmul(GB[1][:, 0:n], lhsT=wg_sb[:, 1, j, :], rhs=XB[:, 0:n], start=True, stop=True),
                 reads=["wg", c.xbn], writes=["G1"])

        def a_s2a(c):
            j, n = c.j, c.B.ntok
            c.AA, c.aan = R_AA.next()
            c.II, c.iin = R_II.next()
            c.SS, c.ssn = R_SS.next()
            AA, II, SS, XC = c.AA, c.II, c.SS, c.XC
            P.op("act", lambda: ACT.activation(out=AA[:, 0:n], in_=GB[0][:, 0:n], func=AF.Tanh, scale=0.5,
                                               bias=cst[:, j:j + 1]), reads=["G0", "cst_hb"], writes=[c.aan])
            P.op("act", lambda: ACT.activation(out=II[:, 0:n], in_=GB[1][:, 0:n], func=AF.Tanh, scale=0.5,
                                               bias=cst[:, 8 + j:9 + j]), reads=["G1", "cst_hb"], writes=[c.iin])
            P.op("dve", lambda: V.scalar_tensor_tensor(out=II[:, 0:n], in0=II[:, 0:n], scalar=1.0, in1=XC[:, 0:n],
                                                       op0=ALU.add, op1=ALU.mult), reads=[c.iin, c.xcn], writes=[c.iin])

        def a_exp(c):
            j, n = c.j, c.B.ntok
            AA = c.AA
            P.op("act", lambda: ACT.activation(out=AA[:, 0:n], in_=AA[:, 0:n], func=AF.Exp,
                                               scale=cst[:, 16 + j:17 + j], bias=cst[:, 16 + j:17 + j]),
                 reads=[c.aan, "cst_hc"], writes=[c.aan])

        def a_sq_pair(cs):
            n = cs[0].B.ntok
            assert [c.aan for c in cs] == ["AA0", "AA1"] and [c.ssn for c in cs] == ["SS0", "SS1"] and [c.iin for c in cs] == ["II0", "II1"]
            P.op("act", lambda: ACT.activation(out=R_SS.pair(n), in_=R_AA.pair(n), func=AF.Square),
                 reads=["AA0", "AA1"], writes=["SS0", "SS1"])

        def a_sqrt_pair(cs):
            n = cs[0].B.ntok
            P.op("act", lambda: ACT.activation(out=R_SS.pair(n), in_=R_SS.pair(n), func=AF.Sqrt, scale=-0.25, bias=0.25),
                 reads=["SS0", "SS1"], writes=["SS0", "SS1"])

        def a_bmult_pair(cs):
            n = cs[0].B.ntok
            P.op("dve", lambda: V.tensor_tensor(out=R_II.pair(n), in0=R_II.pair(n), in1=R_SS.pair(n), op=ALU.mult),
                 reads=["II0", "II1", "SS0", "SS1"], writes=["II0", "II1"])

        def a_scan(c):
            B, j = c.B, c.j
            n, ns, ts = B.ntok, B.nseq, B.ts
            AA, II, SS, SG = c.AA, c.II, c.SS, c.SG
            for s in range(ns):
                P.op("dve", (lambda s=s: V.tensor_tensor_scan(
                    out=SS[:, s * ts:(s + 1) * ts], data0=AA[:, s * ts:(s + 1) * ts],
                    data1=II[:, s * ts:(s + 1) * ts], initial=B.hst[:, j, s:s + 1], op0=ALU.mult, op1=ALU.add)),
                    reads=[c.aan, c.iin, B.hst_n + str(j)], writes=[c.ssn])
            yn = "yT%d_%d" % (B.buf, j)
            ywritten.add((B.buf, j))
            P.op("pool", lambda: PO.tensor_tensor(out=yT[B.buf][:, j * NT:j * NT + n], in0=SS[:, 0:n], in1=SG[:, 0:n], op=ALU.mult),
                 reads=[c.ssn, c.sgn], writes=[yn, "yT%d" % B.buf])
            P.op("pool", lambda: PO.tensor_copy(out=B.hst[:, j, :], in_=v3(SS[:, 0:n], ns)[:, :, ts - 1]),
                 reads=[c.ssn], writes=[B.hst_n + str(j), B.hst_n])

        def b1_s1(B, j):
            c = Ctx()
            c.B, c.j = B, j
            n, ns, ts = B.ntok, B.nseq, B.ts
            W = ns * (2 + ts)
            ext, en = R_ext.next()
            enh, enb = en + "h", en + "b"
            CG, cgn = R_CG.next()
            c.SS, c.ssn = R_UB.next()
            SS = c.SS
            e3 = v3(ext[:, 0:W], ns)
            bc, bcn = ip(B, 24 + j)
            bxb, bxbn = ip(B, 16 + j)
            P.op("act", lambda: ACT.activation(out=CG[:, 0:n], in_=bc[:, 0:n], func=AF.Copy), reads=[bcn], writes=[cgn])
            P.op("dve", lambda: V.tensor_copy(out=e3[:, :, 0:2], in_=B.vin[:, j, :, :]), reads=[B.vin_n], writes=[enh])
            P.op("dve", lambda: V.tensor_tensor(out=e3[:, :, 2:2 + ts], in0=v3(bxb[:, 0:n], ns), in1=v3(CG[:, 0:n], ns),
                                                op=ALU.mult), reads=[bxbn, cgn], writes=[enb])
            P.op("dve", lambda: V.tensor_scalar(out=v3(SS[:, 0:n], ns), in0=e3[:, :, 2:2 + ts],
                                                scalar1=prm[:, C_WCB + 8 + j:C_WCB + 9 + j], scalar2=None, op0=ALU.mult),
                 reads=[enb, "prm"], writes=[c.ssn])
            for k in (1, 0):
                P.op("dve", (lambda k=k: V.scalar_tensor_tensor(
                    out=v3(SS[:, 0:n], ns), in0=e3[:, :, k:k + ts], scalar=prm[:, C_WCB + k * 4 + j:C_WCB + k * 4 + j + 1],
                    in1=v3(SS[:, 0:n], ns), op0=ALU.mult, op1=ALU.add)), reads=[enh, enb, c.ssn, "prm"], writes=[c.ssn])
            P.op("pool", lambda: PO.tensor_copy(out=B.vout[:, j, :, :], in_=e3[:, :, ts:ts + 2]), reads=[enb], writes=[B.vout_n])
            return c

        def b2_s1(c):
            B, j, n = c.B, c.j, c.B.ntok
            SS = c.SS
            SG, sgn = R_SGB.next()
            bb, bbn = ip(B, 20 + j)
            bgb, bgbn = ip(B, 28 + j)
            P.op("act", lambda: ACT.activation(out=SG[:, 0:n], in_=bgb[:, 0:n], func=AF.Silu), reads=[bgbn], writes=[sgn])
            P.op("dve", lambda: V.tensor_tensor(out=SS[:, 0:n], in0=bb[:, 0:n], in1=SS[:, 0:n], op=ALU.mult),
                 reads=[bbn, c.ssn], writes=[c.ssn])
            yn = "yT%d_%d" % (B.buf, 8 + j)
            P.op("pool", lambda: PO.tensor_tensor(out=yT[B.buf][:, (8 + j) * NT:(8 + j) * NT + n], in0=SS[:, 0:n], in1=SG[:, 0:n],
                                                  op=ALU.mult), reads=[c.ssn, sgn], writes=[yn, "yT%d" % B.buf])

        def out_load(B, tt):
            sl = xr_next()
            B.oslots = getattr(B, "oslots", {})
            B.oslots[tt] = sl
            if B.is_sample:
                B.tslot = xr_next()
            np_ = B.np_
            r0 = B.row0 + tt * 128
            P.dma("sp", "ld_xr%d" % sl, lambda: nc.sync.dma_start(out=XR[sl][0:np_, :], in_=B.xsrc[r0:r0 + np_, :]),
                  writes=["XR%d" % sl])

        O0 = O

        def out_half(B, tt, hh):
            sl = B.oslots[tt]
            np_ = B.np_
            O, O_n = (O0, "O") if hh == 0 else (T, "T")
            for c in range(12):
                P.op("pe", (lambda c=c: PE.matmul(O[0:np_, :], lhsT=yT[B.buf][:, c * NT + tt * 128:c * NT + tt * 128 + np_],
                                                  rhs=w_out_sb[:, c, hh * 512:(hh + 1) * 512], start=(c == 0), stop=(c == 11))),
                     reads=["yT%d_%d" % (B.buf, c), "w_out"], writes=[O_n])
            if B.is_sample:
                tl = B.tslot
                assert (1, 2 * hh) not in ywritten and (1, 2 * hh + 1) not in ywritten, "gate_bcS clobbered by schedule"
                P.op("dve", lambda: V.tensor_tensor(out=XR[tl][0:np_, hh * 512:(hh + 1) * 512], in0=O[0:np_, :],
                                                    in1=gate_bcS[0:np_, hh * 512:(hh + 1) * 512], op=ALU.mult),
                     reads=[O_n, gate_bcS_n], writes=["XR%d" % tl])
            else:
                P.op("dve", lambda: V.tensor_tensor(out=XR[sl][0:np_, hh * 512:(hh + 1) * 512], in0=O[0:np_, :],
                                                    in1=XR[sl][0:np_, hh * 512:(hh + 1) * 512], op=ALU.add),
                     reads=[O_n, "XR%d" % sl], writes=["XR%d" % sl])

        def out_fin(B, tt):
            sl = B.oslots[tt]
            np_ = B.np_
            col = cnt["col"] % 16
            cnt["col"] += 1
            xrn = "XR%d" % sl
            r0 = B.row0 + tt * 128
            if B.is_sample:
                tl = B.tslot
                P.op("pool", lambda: PO.tensor_tensor(out=XR[sl][0:np_, :], in0=XR[sl][0:np_, :], in1=XR[tl][0:np_, :], op=ALU.add),
                     reads=[xrn, "XR%d" % tl], writes=[xrn])
            P.op("act", lambda: ACT.activation(out=junk[0:np_, :], in_=XR[sl][0:np_, :], func=AF.Square,
                                               accum_out=stats[0:np_, 0, col:col + 1]),
                 reads=[xrn], writes=["CG0", "ssq%d" % col])
            P.op("dve", lambda: V.tensor_scalar(out=stats[0:np_, 1, col:col + 1], in0=stats[0:np_, 0, col:col + 1],
                                                scalar1=1.0 / D, scalar2=EPS, op0=ALU.mult, op1=ALU.add),
                 reads=["ssq%d" % col], writes=["msq%d" % col])
            P.op("pool", lambda: PO.tensor_tensor(out=stats[0:np_, 2, col:col + 1], in0=stats[0:np_, 1, col:col + 1],
                                                  in1=mhalf[0:np_, col:col + 1], op=ALU.pow),
                 reads=["msq%d" % col, "mhalf"], writes=["rstd%d" % col])
            P.op("dve", lambda: V.scalar_tensor_tensor(out=XR[sl][0:np_, :], in0=XR[sl][0:np_, :],
                                                       scalar=stats[0:np_, 2, col:col + 1], in1=gfin_bc[0:np_, :],
                                                       op0=ALU.mult, op1=ALU.mult),
                 reads=[xrn, "rstd%d" % col, "gfin_bc"], writes=[xrn])
            P.dma("pool", "st_y%d" % sl, lambda: PO.dma_start(out=B.ydst[r0:r0 + np_, :], in_=XR[sl][0:np_, :]),
                  reads=[xrn])

        def fold(c):
            k = 1 + (c % 2)
            stg, stg_n = XR[k], "XR%d" % k
            P.dma("sp", "ld_fold_" + stg_n, lambda: nc.sync.dma_start(out=stg[:, :], in_=w_out[c * 128:(c + 1) * 128, :]),
                  writes=[stg_n])
            P.op("dve", lambda: V.tensor_tensor(out=w_out_sb[:, c, :], in0=stg[:, :], in1=gate_bcP[:, :], op=ALU.mult),
                 reads=[stg_n, gate_bcP_n], writes=["w_out"])

        def state_out(src_fn, src_n, nrows, nchunks, dst, stream):
            sl = xr_next()
            for c in range(nchunks):
                P.op("pe", (lambda c=c: PE.transpose(out=T[0:nrows, (c % 4) * 128:(c % 4 + 1) * 128], in_=src_fn(c),
                                                     identity=idf[:, :])), reads=[src_n, "idf"], writes=["T"])
                if c % 4 == 3 or c == nchunks - 1:
                    c0 = (c // 4) * 4
                    w = (c - c0 + 1) * 128
                    P.op("dve", (lambda c0=c0, w=w: V.tensor_copy(out=XR[sl][0:nrows, c0 * 128:c0 * 128 + w], in_=T[0:nrows, 0:w])),
                         reads=["T"], writes=["XR%d" % sl])
            P.dma("sp", stream + "_%d" % sl, lambda: nc.sync.dma_start(out=dst, in_=XR[sl][0:nrows, 0:nchunks * 128]),
                  reads=["XR%d" % sl])

        def run_side(it):
            if it[0] == "T":
                t_group(it[1], it[2])
            elif it[0] == "OL":
                out_load(it[1], it[2])
            elif it[0] == "OH":
                out_half(it[1], it[2], it[3])
            elif it[0] == "OF":
                out_fin(it[1], it[2])
            elif it[0] == "FOLD":
                fold(it[1])
            elif it[0] == "SO":
                it[1]()

        LAG = 2
        T_UNITS = (3, 5, 7, 9, 11, 13, 14, 15)
        SIDE_O_START = 3
        nblk = len(blocks)
        in_load(blocks[0])
        in_norm(blocks[0])
        allbanks = [(T, "T"), (O, "O"), (PB[0], "P0"), (PB[1], "P1"), (PB[2], "P2"), (PB[3], "P3"), (GB[0], "G0"), (GB[1], "G1")]
        for kc in range(KC):
            t_group(blocks[0], kc, allbanks[kc])
        if nblk > 1:
            in_load(blocks[1])
        for i, B in enumerate(blocks):
            nxt2 = blocks[i + 2] if i + 2 < nblk else None
            nxt = blocks[i + 1] if i + 1 < nblk else None
            prv = blocks[i - 1] if i >= 1 else None
            units = []
            for j in range(4):
                units += [("A", 2 * j), ("A", 2 * j + 1), ("B1", j), ("B2", j)]
            side_t, side_o = [], []
            if nxt is not None:
                side_t = [("T", nxt, kc) for kc in range(KC)]
            if prv is not None:
                for tt in range(prv.ntt):
                    side_o += [("OL", prv, tt), ("OH", prv, tt, 0), ("OH", prv, tt, 1), ("OF", prv, tt)]
                if prv.is_sample:
                    side_o += [("FOLD", c) for c in range(12)]
                    side_o.append(("SO", lambda: state_out(lambda c: hist_s2[:, c, :, :].rearrange("p s k -> p (s k)"),
                                                           "hist_s2", NS * 3, 8, o_cas, "st_s3")))
                    side_o.append(("SO", lambda: state_out(lambda c: hst_s[:, c, :], "hst_s", NS, 8, o_lrs, "st_s4")))
                    side_o.append(("SO", lambda: state_out(lambda c: vhist_s2[:, c, :, :].rearrange("p s k -> p (s k)"),
                                                           "vhist_s2", NS * 2, 4, o_cbs, "st_s5")))
            wait_g, wait_sq, wait_sc, bctx = [], [], [], {}
            for u, (kind, j) in enumerate(units + [("X", 0), ("X", 0)]):
                if wait_sc:
                    a_bmult_pair(wait_sc)
                for c in wait_sc:
                    a_scan(c)
                wait_sc = []
                while wait_g and wait_g[0].u + LAG <= u:
                    c = wait_g.pop(0)
                    a_gates(c)
                    a_s2a(c)
                    wait_sq.append(c)
                if len(wait_sq) >= 2:
                    for c in wait_sq:
                        a_exp(c)
                    a_sq_pair(wait_sq)
                    a_sqrt_pair(wait_sq)
                    wait_sc = wait_sq
                    wait_sq = []
                if kind == "X":
                    continue
                if side_t and u in T_UNITS:
                    run_side(side_t.pop(0))
                    if not side_t and nxt2 is not None:
                        in_load(nxt2)
                        nxt2 = None
                if u >= SIDE_O_START:
                    nside = 2 if len(side_o) > (15 - u) else 1
                    for _ in range(nside):
                        if side_o:
                            run_side(side_o.pop(0))
                if kind == "A":
                    c = a_s1(B, j)
                    c.u = u
                    wait_g.append(c)
                elif kind == "B1":
                    bctx[j] = b1_s1(B, j)
                else:
                    b2_s1(bctx[j])
                if nxt is not None and u == 0:
                    in_norm(nxt)
            assert not wait_g and not wait_sq and not wait_sc
            while side_t:
                run_side(side_t.pop(0))
            if nxt2 is not None:
                in_load(nxt2)
            while side_o:
                run_side(side_o.pop(0))
            if B.name == "p%d" % (NBP - 1):
                hf, vf = hist_p[NBP % 2], vhist_p[NBP % 2]
                state_out(lambda c: hf[:, c, 0, :], "hist_p%d" % (NBP % 2), 3, 8, o_cap, "st_s0")
                state_out(lambda c: hst_p[:, c, :], "hst_p", 1, 8, o_lrp, "st_s1")
                state_out(lambda c: vf[:, c, 0, :], "vhist_p%d" % (NBP % 2), 2, 4, o_cbp, "st_s2")
        B = blocks[-1]
        for tt in range(B.ntt):
            out_load(B, tt)
            out_half(B, tt, 0)
            out_half(B, tt, 1)
            out_fin(B, tt)

        finals = [s for s in P.streams if s.startswith("st_")]
        P.emit(final_streams=finals)
    return nc


_NC_CACHE = {}


def _get_nc(TP):
    if TP not in _NC_CACHE:
        _NC_CACHE[TP] = build_nc(TP)
    return _NC_CACHE[TP]


def _f32(a):
    return np.ascontiguousarray(np.asarray(a, dtype=np.float32))


def kernel(x_prompt, x_sample, c_prompt, c_sample, cache_conv_a, state_lru, cache_conv_b,
           g_norm, w_ada, b_ada, w_in, w_conv_a, b_conv_a, w_rg, b_rg, w_ig, b_ig, lam,
           w_conv_b, w_out, g_final, _TP=None):
    x_prompt = _f32(x_prompt)
    TP = x_prompt.shape[1] if _TP is None else _TP
    nc = _get_nc(TP)
    x_sample, c_prompt, c_sample = _f32(x_sample), _f32(c_prompt), _f32(c_sample)
    cache_conv_a, state_lru, cache_conv_b = _f32(cache_conv_a), _f32(state_lru), _f32(cache_conv_b)
    pv = np.concatenate([
        _f32(g_norm).reshape(8, 128), _f32(w_conv_a).reshape(32, 128), _f32(b_conv_a).reshape(8, 128),
        _f32(b_rg).reshape(8, 128), _f32(b_ig).reshape(8, 128), _f32(lam).reshape(8, 128),
        _f32(w_conv_b).reshape(12, 128), _f32(b_ada).reshape(24, 128)], axis=0)
    assert pv.shape == (NPV, 128)

    def blockdiag(w):
        w = _f32(w).reshape(16, 64, 64)
        o = np.zeros((128, 8, 128), np.float32)
        for c in range(8):
            o[0:64, c, 0:64] = w[2 * c]
            o[64:128, c, 64:128] = w[2 * c + 1]
        return o

    wrg_bd, wig_bd = blockdiag(w_rg), blockdiag(w_ig)
    shared = {
        "pv": np.ascontiguousarray(pv), "b_ada": _f32(b_ada).reshape(1, 3 * D), "g_fin": _f32(g_final).reshape(1, D),
        "w_ada": _f32(w_ada).reshape(D, 3 * D), "w_in": np.ascontiguousarray(_f32(w_in).reshape(D, 32, 128)[:, W_ORDER, :].reshape(D, 4096)), "w_out": _f32(w_out).reshape(1536, D),
        "wrg": wrg_bd, "wig": wig_bd,
    }
    in_maps = []
    for i in range(NCORES):
        m = dict(shared)
        m["xp"] = np.ascontiguousarray(x_prompt[i, :TP])
        m["xs"] = np.ascontiguousarray(x_sample[NS * i:NS * (i + 1)].reshape(NS * TS, D))
        m["c5"] = np.ascontiguousarray(np.concatenate([c_prompt[i:i + 1], c_sample[NS * i:NS * (i + 1)]], axis=0))
        m["cca"] = np.ascontiguousarray(cache_conv_a[0, NS * i:NS * (i + 1)].reshape(NS * 3, D))
        m["slru"] = np.ascontiguousarray(state_lru[0, NS * i:NS * (i + 1)])
        m["ccb"] = np.ascontiguousarray(cache_conv_b[0, NS * i:NS * (i + 1)].reshape(NS * 2, 512))
        in_maps.append(m)
    res = run_bass_kernel_spmd(nc, in_maps, core_ids=list(range(NCORES)))
    R = res.results
    y_prompt = np.stack([R[i]["yp"] for i in range(NCORES)], axis=0)
    y_sample = np.concatenate([R[i]["ys"].reshape(NS, TS, D) for i in range(NCORES)], axis=0)
    conv_a_prompt = np.stack([R[i]["o_cap"] for i in range(NCORES)], axis=0)[None]
    lru_prompt = np.stack([R[i]["o_lrp"].reshape(D) for i in range(NCORES)], axis=0)[None]
    conv_b_prompt = np.stack([R[i]["o_cbp"] for i in range(NCORES)], axis=0)[None]
    conv_a_sample = np.concatenate([R[i]["o_cas"].reshape(NS, 3, D) for i in range(NCORES)], axis=0)[None]
    lru_sample = np.concatenate([R[i]["o_lrs"] for i in range(NCORES)], axis=0)[None]
    conv_b_sample = np.concatenate([R[i]["o_cbs"].reshape(NS, 2, 512) for i in range(NCORES)], axis=0)[None]
    outs = (y_prompt, y_sample, conv_a_prompt, lru_prompt, conv_b_prompt, conv_a_sample, lru_sample, conv_b_sample)
    return tuple(np.ascontiguousarray(o.astype(np.float32)) for o in outs)
```
